# Optimizing a Trainium2 kernel written in Bass

```python
import jax, jax.numpy as jnp
from jax import lax
import numpy as np

D_MODEL = 4096
BATCH = 2
SEQ = 4096
DEPTH = 1

HEAD_DIM = 64
MIX_WIDTH = D_MODEL
RWKV_WIDTH = 3 * D_MODEL // 8
FOX_WIDTH = 3 * D_MODEL // 8
MEM_WIDTH = D_MODEL // 4
RWKV_HEADS = RWKV_WIDTH // HEAD_DIM
FOX_HEADS = FOX_WIDTH // HEAD_DIM
MEM_HEADS = 4
MEM_HEAD_DIM = MEM_WIDTH // MEM_HEADS
N_MEM = 256
DECAY_LORA = 128
ICLR_LORA = 128
Q_BLOCK = 128
RMS_EPS = 1e-6
GN_EPS = 64e-5

RWKV_SHIFT_SPLITS = [RWKV_WIDTH, RWKV_WIDTH, RWKV_WIDTH, DECAY_LORA, ICLR_LORA]
RWKV_SHIFT_WIDTH = 3 * RWKV_WIDTH + DECAY_LORA + ICLR_LORA
REST_SPLITS = [RWKV_WIDTH, FOX_WIDTH, FOX_WIDTH, FOX_WIDTH, FOX_HEADS, FOX_WIDTH, MEM_WIDTH, MEM_WIDTH]
REST_WIDTH = RWKV_WIDTH + 4 * FOX_WIDTH + FOX_HEADS + 2 * MEM_WIDTH
IN_WIDTH = RWKV_SHIFT_WIDTH + REST_WIDTH

kernel_name = "hymba_rwkv7_fox_memxattn_layer"


def _offsets(sizes):
    return [int(s) for s in np.cumsum(sizes)[:-1]]


def rms_norm(x, g):
    xf = x.astype(jnp.float32)
    y = xf * lax.rsqrt(jnp.mean(xf * xf, axis=-1, keepdims=True) + RMS_EPS)
    return (y * g.astype(jnp.float32)).astype(x.dtype)


def token_shift(p, mu):
    prev = jnp.pad(p, ((0, 0), (1, 0), (0, 0)))[:, :-1]
    return p + (prev - p) * mu


def rwkv7_branch(r, k, v, wl, al, w0, w_decay_up, a0, w_iclr_up, k_k, k_a, r_k, ln_x_w, ln_x_b):
    B, T, _ = r.shape
    H, N = RWKV_HEADS, HEAD_DIM
    f32 = jnp.float32
    w_pre = -jax.nn.softplus(-(w0 + jnp.tanh(wl) @ w_decay_up).astype(f32)) - 0.5
    decay = jnp.exp(-jnp.exp(w_pre))
    alpha = jax.nn.sigmoid((a0 + al @ w_iclr_up).astype(f32))
    kk = (k * k_k).astype(f32).reshape(B, T, H, N)
    kk = kk * lax.rsqrt(jnp.maximum(jnp.sum(kk * kk, axis=-1, keepdims=True), 1e-24))
    k_mod = k.astype(f32) * (1.0 + (alpha - 1.0) * k_a)

    def heads(z):
        return z.astype(f32).reshape(B, T, H, N)

    r_h, k_h, v_h, w_h, al_h = heads(r), heads(k_mod), heads(v), heads(decay), heads(alpha)
    a_h = -kk
    b_h = kk * al_h

    def step(S, inp):
        r_t, w_t, k_t, v_t, a_t, b_t = inp
        Sa = jnp.einsum('bhij,bhj->bhi', S, a_t)
        S = S * w_t[:, :, None, :] + Sa[..., None] * b_t[:, :, None, :] + v_t[..., None] * k_t[:, :, None, :]
        y_t = jnp.einsum('bhij,bhj->bhi', S, r_t)
        return S, y_t

    tm = lambda z: jnp.moveaxis(z, 1, 0)
    S0 = jnp.zeros((B, H, N, N), f32)
    _, y = lax.scan(step, S0, (tm(r_h), tm(w_h), tm(k_h), tm(v_h), tm(a_h), tm(b_h)))
    y = jnp.moveaxis(y, 0, 1)
    mean = jnp.mean(y, axis=-1, keepdims=True)
    var = jnp.mean(jnp.square(y - mean), axis=-1, keepdims=True)
    y = (y - mean) * lax.rsqrt(var + GN_EPS) * ln_x_w.astype(f32).reshape(H, N) + ln_x_b.astype(f32).reshape(H, N)
    bonus = jnp.sum(r_h * k_h * r_k.astype(f32), axis=-1, keepdims=True) * v_h
    return (y + bonus).reshape(B, T, H * N).astype(r.dtype)


def fox_attention(q, k, v, log_f):
    B, H, T, D = q.shape
    n_blk = T // Q_BLOCK
    cum = jnp.cumsum(log_f.astype(jnp.float32), axis=-1)
    kpos = jnp.arange(T)
    scale = D ** -0.5

    def one_block(i):
        start = i * Q_BLOCK
        qb = lax.dynamic_slice_in_dim(q, start, Q_BLOCK, axis=2)
        cb = lax.dynamic_slice_in_dim(cum, start, Q_BLOCK, axis=2)
        s = jnp.einsum('bhqd,bhkd->bhqk', qb, k).astype(jnp.float32) * scale + cb[..., :, None] - cum[..., None, :]
        qpos = start + jnp.arange(Q_BLOCK)
        s = jnp.where(kpos[None, :] <= qpos[:, None], s, -jnp.inf)
        p = jax.nn.softmax(s, axis=-1)
        return jnp.einsum('bhqk,bhkd->bhqd', p.astype(v.dtype), v)

    out = lax.map(one_block, jnp.arange(n_blk))
    return jnp.moveaxis(out, 0, 2).reshape(B, H, T, D)


def memory_cross_attention(q, mem, g_mem, w_mem_kv):
    B, T, _ = q.shape
    M = mem.shape[1]
    mkv = rms_norm(mem, g_mem) @ w_mem_kv
    mk, mv = jnp.split(mkv, 2, axis=-1)
    qh = q.reshape(B, T, MEM_HEADS, MEM_HEAD_DIM)
    mk = mk.reshape(B, M, MEM_HEADS, MEM_HEAD_DIM)
    mv = mv.reshape(B, M, MEM_HEADS, MEM_HEAD_DIM)
    s = jnp.einsum('bthd,bmhd->bhtm', qh, mk).astype(jnp.float32) * (MEM_HEAD_DIM ** -0.5)
    p = jax.nn.softmax(s, axis=-1)
    o = jnp.einsum('bhtm,bmhd->bthd', p.astype(mv.dtype), mv)
    return o.reshape(B, T, MEM_WIDTH)


def hybrid_layer(x, mem, g_pre, w_in, mu_rwkv, w0, w_decay_up, a0, w_iclr_up, k_k, k_a, r_k,
                 ln_x_w, ln_x_b, b_f, g_mem, w_mem_kv, w_out, g_post):
    B, T, _ = x.shape
    h = rms_norm(x, g_pre)
    p = h @ w_in
    p_shift = token_shift(p[..., :RWKV_SHIFT_WIDTH], mu_rwkv)
    r, k, v, wl, al = jnp.split(p_shift, _offsets(RWKV_SHIFT_SPLITS), axis=-1)
    g_rwkv, fq, fk, fv, f_logit, g_fox, mq, g_mq = jnp.split(p[..., RWKV_SHIFT_WIDTH:], _offsets(REST_SPLITS), axis=-1)

    y_rwkv = rwkv7_branch(r, k, v, wl, al, w0, w_decay_up, a0, w_iclr_up, k_k, k_a, r_k, ln_x_w, ln_x_b)

    to_heads = lambda z: z.reshape(B, T, FOX_HEADS, HEAD_DIM).transpose(0, 2, 1, 3)
    log_f = jax.nn.log_sigmoid((f_logit + b_f).astype(jnp.float32)).transpose(0, 2, 1)
    y_fox = fox_attention(to_heads(fq), to_heads(fk), to_heads(fv), log_f)
    y_fox = y_fox.transpose(0, 2, 1, 3).reshape(B, T, FOX_WIDTH)

    y_mem = memory_cross_attention(mq, mem, g_mem, w_mem_kv)

    y = jnp.concatenate([y_rwkv * jax.nn.silu(g_rwkv),
                         y_fox * jax.nn.silu(g_fox),
                         y_mem * jax.nn.silu(g_mq)], axis=-1)
    y = y @ w_out
    return x + rms_norm(y, g_post)


def setup_inputs(seed: int = 0) -> dict:
    key = jax.random.key(seed)
    ks = jax.random.split(key, 20)
    f32 = jnp.float32
    nrm = lambda k, shape, s: jax.random.normal(k, shape, f32) * s
    n = jnp.arange(RWKV_WIDTH, dtype=f32) / (RWKV_WIDTH - 1)
    return {
        "x": nrm(ks[0], (BATCH, SEQ, D_MODEL), 1.0),
        "mem": nrm(ks[1], (BATCH, N_MEM, D_MODEL), 1.0),
        "g_pre": 1.0 + nrm(ks[2], (DEPTH, D_MODEL), 0.02),
        "w_in": nrm(ks[3], (DEPTH, D_MODEL, IN_WIDTH), D_MODEL ** -0.5),
        "mu_rwkv": jax.random.uniform(ks[4], (DEPTH, RWKV_SHIFT_WIDTH), f32),
        "w0": (-5.5 + 5.0 * n ** 0.85)[None, :] + nrm(ks[5], (DEPTH, RWKV_WIDTH), 0.1),
        "w_decay_up": nrm(ks[6], (DEPTH, DECAY_LORA, RWKV_WIDTH), DECAY_LORA ** -0.5),
        "a0": nrm(ks[7], (DEPTH, RWKV_WIDTH), 0.1),
        "w_iclr_up": nrm(ks[8], (DEPTH, ICLR_LORA, RWKV_WIDTH), ICLR_LORA ** -0.5),
        "k_k": 0.85 + nrm(ks[9], (DEPTH, RWKV_WIDTH), 0.02),
        "k_a": 1.0 + nrm(ks[10], (DEPTH, RWKV_WIDTH), 0.02),
        "r_k": -0.04 + nrm(ks[11], (DEPTH, RWKV_HEADS, HEAD_DIM), 0.02),
        "ln_x_w": 1.0 + nrm(ks[12], (DEPTH, RWKV_WIDTH), 0.02),
        "ln_x_b": nrm(ks[13], (DEPTH, RWKV_WIDTH), 0.02),
        "b_f": 2.0 + nrm(ks[14], (DEPTH, FOX_HEADS), 0.5),
        "g_mem": 1.0 + nrm(ks[15], (DEPTH, D_MODEL), 0.02),
        "w_mem_kv": nrm(ks[16], (DEPTH, D_MODEL, 2 * MEM_WIDTH), D_MODEL ** -0.5),
        "w_out": nrm(ks[17], (DEPTH, MIX_WIDTH, D_MODEL), MIX_WIDTH ** -0.5),
        "g_post": 1.0 + nrm(ks[18], (DEPTH, D_MODEL), 0.02),
    }


def reference(x, mem, g_pre, w_in, mu_rwkv, w0, w_decay_up, a0, w_iclr_up, k_k, k_a, r_k,
              ln_x_w, ln_x_b, b_f, g_mem, w_mem_kv, w_out, g_post):
    for l in range(DEPTH):
        x = hybrid_layer(x, mem, g_pre[l], w_in[l], mu_rwkv[l], w0[l], w_decay_up[l], a0[l],
                         w_iclr_up[l], k_k[l], k_a[l], r_k[l], ln_x_w[l], ln_x_b[l], b_f[l],
                         g_mem[l], w_mem_kv[l], w_out[l], g_post[l])
    return x
```

```python
import os
import contextlib
import numpy as np
import concourse.bass as bass
import concourse.mybir as mybir
from concourse.bass_utils import run_bass_kernel_spmd

F32 = mybir.dt.float32
BF16 = mybir.dt.bfloat16
AF = mybir.ActivationFunctionType
ALU = mybir.AluOpType
AX = mybir.AxisListType

ENGS = ("pe", "act", "dve", "pool", "sp")
T = 4096
DM = 4096
NKC = 32
UW = [262, 512, 512, 512, 512, 512, 512, 512]
UOFF = [0]
for _w in UW:
    UOFF.append(UOFF[-1] + _w)
NWC = UOFF[-1]
EXPM05 = 0.6065306597126334


class Res:
    __slots__ = ("w", "r", "name")

    def __init__(self, name=""):
        self.w = None
        self.r = {}
        self.name = name


class Sched:
    def __init__(self, nc):
        self.nc = nc
        self.ops = {e: [] for e in ENGS}
        self.cnt = {e: 0 for e in ENGS}
        self.seen = {e: {f: 0 for f in ENGS} for e in ENGS}
        self.sem = {}
        self.dma_sems = []
        self._dma_rr = 0

    def _deps(self, e, reads, writes):
        need = {}

        def add(f, c, war=False):
            if f == e and (e == "pe" or e == "sp" or war):
                return
            if need.get(f, 0) < c:
                need[f] = c
        for r in reads:
            if r.w is not None:
                add(*r.w)
        for w in writes:
            if w.w is not None:
                add(*w.w)
            for f, c in w.r.items():
                add(f, c, True)
        return need

    def _emit_waits(self, e, need):
        for f, c in need.items():
            if self.seen[e].get(f, 0) >= c:
                continue
            self.seen[e][f] = c
            sem = self.dma_sems[f[1]][0] if isinstance(f, tuple) else self.sem[f]
            self.ops[e].append(lambda en, sem=sem, c=c: en.wait_ge(sem, c))

    def op(self, e, fn, reads=(), writes=()):
        pr = [r for r in reads if r.name.startswith("ps")]
        if pr:
            reads = [r for r in reads if not r.name.startswith("ps")]
            writes = list(writes) + [r for r in pr if r not in writes]
        self._emit_waits(e, self._deps(e, reads, writes))
        self.cnt[e] += 1
        c = self.cnt[e]
        sem = self.sem[e]
        self.ops[e].append(lambda en, fn=fn, sem=sem: fn(en).then_inc(sem, 1))
        for r in reads:
            r.r[e] = c
        for w in writes:
            w.w = (e, c)
            w.r = {}

    def dma(self, e, fn, reads=(), writes=()):
        self._emit_waits(e, self._deps(e, reads, writes))
        n = len(self.dma_sems)
        idx = self._dma_rr
        self._dma_rr = (idx + 1) % n
        ent = self.dma_sems[idx]
        key = ("dma", idx)
        if ent[1] > 0 and self.seen[e].get(key, 0) < ent[1]:
            self._emit_waits(e, {key: ent[1]})
        ent[1] += 16
        val = ent[1]
        sem = ent[0]
        self.ops[e].append(lambda en, fn=fn, sem=sem: fn(en).then_inc(sem, 16))
        for r in reads:
            r.r[key] = val
        for w in writes:
            w.w = (key, val)
            w.r = {}

    def barrier(self):
        for e in ENGS:
            need = {f: self.cnt[f] for f in ("pe", "act", "dve", "pool") if f != e and self.cnt[f] > 0}
            for idx, ent in enumerate(self.dma_sems):
                if ent[1] > 0:
                    need[("dma", idx)] = ent[1]
            self._emit_waits(e, need)


class Arena:
    def __init__(self, ap):
        self.ap = ap
        self.off = 0

    def f32(self, n):
        v = self.ap[:, self.off:self.off + n]
        self.off += n
        return v

    def bf16(self, n):
        m = (n + 1) // 2
        v = self.ap[:, self.off:self.off + m].bitcast(BF16)
        self.off += m
        return v


def build_program(stage=99, G=1):
    DBG = bool(os.environ.get("KDBG"))
    FUSED = G > 1
    nc = bass.Bass("TRN2", target_bir_lowering=False)
    dr = lambda name, shape, dt, kind="ExternalInput": nc.dram_tensor(name, shape, dt, kind=kind).ap()
    xT = dr("xT", [DM, T], F32)
    memT = dr("memT", [DM, 256], F32)
    wc4 = dr("wc", [G, 128, NKC * NWC], F32)
    wkv4 = dr("wkv", [G, 128, NKC * 512], F32)
    cstf = dr("cstf", [128, 896], F32)
    cstb = dr("cstb", [128, 1024], F32)
    prm4 = dr("prm", [G, 128, 114], F32)
    rowp4 = dr("rowp", [G, 1, 774], F32)
    wlora4 = dr("wlora", [G, 128, 768], F32)
    if FUSED:
        xfull = dr("xfull", [T, DM], F32)
        wout = dr("wout", [8, 128, NKC * 512], F32)
        gpost = dr("gpost", [1, DM], F32)
        out = dr("out", [T, DM], F32, kind="ExternalOutput")
    hT_d = dr("hT_d", [DM, T], BF16, kind="ExternalOutput" if DBG else "Internal")
    hmT_d = dr("hmT_d", [DM, 256], BF16, kind="Internal")
    yT_d = dr("yT_d", [1024 * G, T], BF16, kind="Internal" if FUSED else "ExternalOutput")
    gsel = {"g": 0}
    yall_d = dr("yall_d", [4096, T], BF16, kind="Internal")
    lora_d = dr("lora_d", [2, 128, T], BF16, kind="Internal")
    dbg = dr("dbg", [128, 4096], F32, kind="ExternalOutput") if DBG else None

    S = Sched(nc)
    with contextlib.ExitStack() as st:
        for e in ("pe", "act", "dve", "pool"):
            S.sem[e] = st.enter_context(nc.semaphore("s_" + e))
        for i in range(40):
            S.dma_sems.append([st.enter_context(nc.semaphore("d%d" % i)), 0])
        ARENA_N = 43008
        arena_t = st.enter_context(nc.sbuf_tensor("arena", [128, ARENA_N], F32))
        ps = [st.enter_context(nc.psum_tensor("ps%d" % i, [128, 512], F32)) for i in range(8)]
        rps = [Res("ps%d" % i) for i in range(8)]
        A = Arena(arena_t[:, :])

        c_f = A.f32(896); r_cf = Res()
        c_b = A.bf16(1024); r_cb = Res()
        p_f = A.f32(114); r_pf = Res()
        rowb = A.f32(774); r_rowb = Res()
        wl_b = A.bf16(768); r_wlb = Res()
        cum = A.f32(192).rearrange("p (b h) -> p b h", h=6); r_cum = Res()
        refbc = A.f32(192).rearrange("p (b h) -> p b h", h=6); r_ref = Res()
        PERS = A.off

        ident_f = c_f[:, 0:128]
        tri_f = c_f[:, 128:256]
        ones_f = c_f[:, 256:384]
        rmask = c_f[:, 384:896]
        ident_b = c_b[:, 0:128]
        ones_b = c_b[:, 128:256]
        mask_fox = c_b[:, 256:384]
        maskP = c_b[:, 384:896]
        maskN0 = c_b[:, 896:1024]
        gpre = p_f[:, 0:32]
        gmem = p_f[:, 32:64]
        lnw_bc = rowb[:, 0:384]
        lnb_bc = rowb[:, 384:768]
        bf_bc = rowb[:, 768:774]

        S.dma("sp", lambda e: e.dma_start(out=c_f, in_=cstf), writes=[r_cf])
        S.dma("pool", lambda e: e.dma_start(out=c_b, in_=cstb), writes=[r_cb])
        def load_params(g):
            gsel["g"] = g
            S.dma("sp", lambda e: e.dma_start(out=p_f, in_=prm4[g]), writes=[r_pf])
            S.dma("sp", lambda e: e.dma_start(out=rowb, in_=rowp4[g, 0:1, :].to_broadcast([128, 774])), writes=[r_rowb])
            S.dma("pool", lambda e: e.dma_start(out=wl_b, in_=wlora4[g]), writes=[r_wlb])
        load_params(0)

        hTv = hT_d.rearrange("(kc p) t -> p kc t", p=128)
        r_hTd = Res()
        r_hmTd = Res()
        r_yTd = Res()

        A.off = PERS
        xb = [A.f32(NKC * 256).rearrange("p (kc t) -> p kc t", t=256) for _ in range(2)]
        r_xb = [[Res(), Res()] for _ in range(2)]
        sq = A.bf16(NKC * 256).rearrange("p (kc t) -> p kc t", t=256); r_sq = Res()
        hb = [A.bf16(NKC * 256).rearrange("p (kc t) -> p kc t", t=256) for _ in range(2)]
        r_hb = [Res(), Res()]
        rstd = A.f32(256); r_rstd = Res()
        xTv = xT.rearrange("(kc p) t -> p kc t", p=128)
        mTv = memT.rearrange("(kc p) t -> p kc t", p=128)
        hmTv = hmT_d.rearrange("(kc p) t -> p kc t", p=128)

        def p1_load(i):
            b = i % 2
            src = mTv if i == 16 else xTv[:, :, i * 256:(i + 1) * 256]
            for hf in range(2):
                S.dma("sp", lambda e, b=b, hf=hf, src=src: e.dma_start(out=xb[b][:, 16 * hf:16 * hf + 16, :], in_=src[:, 16 * hf:16 * hf + 16, :]),
                      writes=[r_xb[b][hf]])

        p1_load(0)
        for i in range(17):
            b = i % 2
            if i + 1 < 17:
                p1_load(i + 1)
            gvec = gmem if i == 16 else gpre
            for hf in range(2):
                S.op("act", lambda e, b=b, hf=hf: e.activation(out=sq[:, 16 * hf:16 * hf + 16, :], in_=xb[b][:, 16 * hf:16 * hf + 16, :], func=AF.Square),
                     reads=[r_xb[b][hf]], writes=[r_sq])
            for kc in range(NKC):
                S.op("pe", lambda e, kc=kc: e.matmul(ps[0][:, 0:256], lhsT=ones_b, rhs=sq[:, kc, :], start=(kc == 0), stop=(kc == NKC - 1)),
                     reads=[r_sq, r_cb], writes=[rps[0]])
            S.op("dve", lambda e: e.tensor_scalar(out=rstd, in0=ps[0][:, 0:256], scalar1=1.0 / DM, scalar2=1e-6, op0=ALU.mult, op1=ALU.add),
                 reads=[rps[0]], writes=[r_rstd])
            S.op("act", lambda e: e.activation(out=rstd, in_=rstd, func=AF.Sqrt), reads=[r_rstd], writes=[r_rstd])
            S.op("dve", lambda e: e.reciprocal(out=rstd, in_=rstd), reads=[r_rstd], writes=[r_rstd])
            for kc in range(NKC):
                S.op("dve", lambda e, b=b, kc=kc, gvec=gvec: e.scalar_tensor_tensor(out=hb[b][:, kc, :], in0=xb[b][:, kc, :], scalar=gvec[:, kc:kc + 1], in1=rstd,
                                                                                 op0=ALU.mult, op1=ALU.mult),
                     reads=[r_xb[b][kc // 16], r_rstd, r_pf], writes=[r_hb[b]])
            if i == 16:
                S.dma("pool", lambda e, b=b: e.dma_start(out=hmTv, in_=hb[b]), reads=[r_hb[b]], writes=[r_hmTd])
            else:
                S.dma("pool", lambda e, b=b, i=i: e.dma_start(out=hTv[:, :, i * 256:(i + 1) * 256], in_=hb[b]), reads=[r_hb[b]], writes=[r_hTd])
        S.barrier()

        A.off = PERS
        Wb = A.bf16(NKC * 512); r_W = Res()
        hbuf = [A.bf16(NKC * 512).rearrange("p (kc t) -> p kc t", t=512) for _ in range(2)]
        r_hbuf = [[Res(), Res()] for _ in range(2)]
        WORK = A.off

        def load_unit(u, src=None, n=None):
            if src is None:
                n = UW[u]
                src = wc4[gsel["g"], :, NKC * UOFF[u]:NKC * (UOFF[u] + n)]
            tot = NKC * n
            o = 0
            while o < tot:
                m = min(2048, tot - o)
                S.dma("pool", lambda e, o=o, m=m, src=src: e.dma_start(out=Wb[:, o:o + m], in_=src[:, o:o + m]), writes=[r_W])
                o += m
            return Wb[:, 0:tot].rearrange("p (kc n) -> p kc n", n=n)

        hcount = [0]

        def h_load(tb):
            b = hcount[0] % 2
            hcount[0] += 1
            for hf in range(2):
                S.dma("sp", lambda e, b=b, hf=hf, tb=tb: e.dma_start(out=hbuf[b][:, 16 * hf:16 * hf + 16, :], in_=hTv[:, 16 * hf:16 * hf + 16, tb * 512:(tb + 1) * 512]),
                      reads=[r_hTd], writes=[r_hbuf[b][hf]])
            return b

        def proj_fm(Wv, c0, m, hb_i, pst, prs, n0=0, n=512):
            for kc in range(NKC):
                S.op("pe", lambda e, kc=kc: e.matmul(pst[0:m, 0:n], lhsT=Wv[:, kc, c0:c0 + m], rhs=hbuf[hb_i][:, kc, n0:n0 + n], start=(kc == 0), stop=(kc == NKC - 1)),
                     reads=[r_W, r_hbuf[hb_i][kc // 16]], writes=[prs])

        def proj_tm(Wv, c0, m, hb_i, tt, pso, prs):
            for kc in range(NKC):
                S.op("pe", lambda e, kc=kc: e.matmul(pso, lhsT=hbuf[hb_i][:, kc, 128 * tt:128 * tt + 128], rhs=Wv[:, kc, c0:c0 + m], start=(kc == 0), stop=(kc == NKC - 1)),
                     reads=[r_W, r_hbuf[hb_i][kc // 16]], writes=[prs])

        def shift_block(praw, r_praw, mu_ap, npart, outf, r_out, eng_copy="act"):
            S.op("dve", lambda e: e.tensor_tensor(out=outf, in0=praw[0:npart, 0:512], in1=praw[0:npart, 1:513], op=ALU.subtract),
                 reads=[r_praw], writes=[r_out])
            S.op("dve", lambda e: e.scalar_tensor_tensor(out=outf, in0=outf, scalar=mu_ap, in1=praw[0:npart, 1:513], op0=ALU.mult, op1=ALU.add),
                 reads=[r_praw, r_out, r_pf], writes=[r_out])
            S.op(eng_copy, (lambda e: e.activation(out=praw[0:npart, 0:1], in_=praw[0:npart, 512:513], func=AF.Copy)) if eng_copy == "act"
                 else (lambda e: e.tensor_copy(out=praw[0:npart, 0:1], in_=praw[0:npart, 512:513])), reads=[r_praw], writes=[r_praw])

        r_lorad = Res()

        def unit0():
            A.off = WORK
            pr_wl = A.f32(513); r_prwl = Res()
            pr_al = A.f32(513); r_pral = Res()
            tmpf = A.f32(512); r_tmpf = Res()
            lob = [[A.bf16(512) for _ in range(2)] for _ in range(2)]; r_lob = [[Res(), Res()], [Res(), Res()]]
            lfraw = A.f32(192).rearrange("p (b h) -> p b h", h=6); r_lf = Res()
            tot = A.f32(192).rearrange("p (b h) -> p b h", h=6); r_tot = Res()
            blkoff = A.f32(192).rearrange("p (b h) -> p b h", h=6); r_boff = Res()
            S.op("dve", lambda e: e.memset(pr_wl[:, 0:1], 0.0), writes=[r_prwl])
            S.op("dve", lambda e: e.memset(pr_al[:, 0:1], 0.0), writes=[r_pral])
            Wv = load_unit(0)
            nb = h_load(0)
            for tb in range(8):
                hb_i = nb
                if tb + 1 < 8:
                    nb = h_load(tb + 1)
                proj_fm(Wv, 0, 128, hb_i, ps[0], rps[0])
                S.op("act", lambda e: e.activation(out=pr_wl[:, 1:513], in_=ps[0][:, :], func=AF.Copy), reads=[rps[0]], writes=[r_prwl])
                proj_fm(Wv, 128, 128, hb_i, ps[1], rps[1])
                S.op("dve", lambda e: e.tensor_copy(out=pr_al[:, 1:513], in_=ps[1][:, :]), reads=[rps[1]], writes=[r_pral])
                for tt in range(4):
                    proj_tm(Wv, 256, 6, hb_i, tt, ps[2][:, 6 * tt:6 * tt + 6], rps[2])
                S.op("act", lambda e, tb=tb: e.activation(out=lfraw[:, 4 * tb:4 * tb + 4, :], in_=ps[2][:, 0:24].rearrange("p (b h) -> p b h", h=6), func=AF.Copy),
                     reads=[rps[2]], writes=[r_lf])
                shift_block(pr_wl, r_prwl, p_f[:, 64:65], 128, tmpf, r_tmpf)
                S.op("act", lambda e, tb=tb: e.activation(out=lob[0][tb % 2], in_=tmpf, func=AF.Tanh), reads=[r_tmpf], writes=[r_lob[0][tb % 2]])
                S.dma("pool", lambda e, tb=tb: e.dma_start(out=lora_d[0, :, tb * 512:(tb + 1) * 512], in_=lob[0][tb % 2]), reads=[r_lob[0][tb % 2]], writes=[r_lorad])
                shift_block(pr_al, r_pral, p_f[:, 65:66], 128, tmpf, r_tmpf)
                S.op("act", lambda e, tb=tb: e.activation(out=lob[1][tb % 2], in_=tmpf, func=AF.Copy), reads=[r_tmpf], writes=[r_lob[1][tb % 2]])
                S.dma("pool", lambda e, tb=tb: e.dma_start(out=lora_d[1, :, tb * 512:(tb + 1) * 512], in_=lob[1][tb % 2]), reads=[r_lob[1][tb % 2]], writes=[r_lorad])
            S.op("dve", lambda e: e.tensor_tensor(out=lfraw, in0=lfraw, in1=bf_bc.unsqueeze(1).to_broadcast([128, 32, 6]), op=ALU.add), reads=[r_lf, r_rowb], writes=[r_lf])
            S.op("act", lambda e: e.activation(out=lfraw, in_=lfraw, func=AF.Sigmoid), reads=[r_lf], writes=[r_lf])
            S.op("act", lambda e: e.activation(out=lfraw, in_=lfraw, func=AF.Ln), reads=[r_lf], writes=[r_lf])
            lf2 = lfraw.rearrange("p b h -> p (b h)")
            S.op("pe", lambda e: e.matmul(ps[0][:, 0:192], lhsT=tri_f, rhs=lf2, start=True, stop=True), reads=[r_lf, r_cf], writes=[rps[0]])
            S.op("pe", lambda e: e.matmul(ps[1][:, 0:192], lhsT=ones_f, rhs=lf2, start=True, stop=True), reads=[r_lf, r_cf], writes=[rps[1]])
            S.op("act", lambda e: e.activation(out=tot.rearrange("p b h -> p (b h)"), in_=ps[1][:, 0:192], func=AF.Copy), reads=[rps[1]], writes=[r_tot])
            S.op("dve", lambda e: e.memset(blkoff[:, 0, :], 0.0), writes=[r_boff])
            for b in range(31):
                S.op("dve", lambda e, b=b: e.tensor_tensor(out=blkoff[:, b + 1, :], in0=blkoff[:, b, :], in1=tot[:, b, :], op=ALU.add), reads=[r_tot, r_boff], writes=[r_boff])
            S.op("dve", lambda e: e.tensor_tensor(out=cum.rearrange("p b h -> p (b h)"), in0=ps[0][:, 0:192], in1=blkoff.rearrange("p b h -> p (b h)"), op=ALU.add),
                 reads=[rps[0], r_boff], writes=[r_cum])
            S.op("dve", lambda e: e.scalar_tensor_tensor(out=refbc.rearrange("p b h -> p (b h)"), in0=tot.rearrange("p b h -> p (b h)"), scalar=0.5,
                                                         in1=blkoff.rearrange("p b h -> p (b h)"), op0=ALU.mult, op1=ALU.add), reads=[r_tot, r_boff], writes=[r_ref])
            if DBG:
                S.dma("sp", lambda e: e.dma_start(out=dbg[:, 0:192], in_=cum.rearrange("p b h -> p (b h)")), reads=[r_cum])
                S.dma("sp", lambda e: e.dma_start(out=dbg[:, 192:384], in_=refbc.rearrange("p b h -> p (b h)")), reads=[r_ref])
                S.dma("sp", lambda e: e.dma_start(out=dbg[:, 384:576], in_=lfraw.rearrange("p b h -> p (b h)")), reads=[r_lf])
            S.barrier()


        def fox_unit(j):
            A.off = WORK
            yr0 = 1024 * gsel["g"]
            qT = A.bf16(T); r_q = Res()
            kT = A.bf16(T); r_k = Res()
            sgT = A.bf16(T); r_sg = Res()
            vaug = A.bf16(32 * 256).rearrange("p (b h d) -> p b h d", h=2, d=128); r_v = Res()
            biasT = A.f32(2 * 32 * 32).rearrange("p (h k q) -> p h k q", h=2, k=32); r_bias = Res()
            Eb = [A.bf16(512) for _ in range(3)]; r_E = [Res() for _ in range(3)]
            dsh = A.f32(512); r_dsh = Res()
            ytmp = A.f32(512); r_ytmp = Res()
            ybuf = [A.bf16(512) for _ in range(2)]; r_yb = [Res(), Res()]
            S.op("dve", lambda e: e.memset(vaug[:, :, 0, 64:128], 1.0), writes=[r_v])
            S.op("dve", lambda e: e.memset(vaug[:, :, 1, 0:64], 1.0), writes=[r_v])
            for h in range(2):
                hh = 2 * j + h
                for kb in range(32):
                    S.op("dve", lambda e, h=h, hh=hh, kb=kb: e.tensor_scalar(out=biasT[:, h, kb, kb:32], in0=refbc[:, kb:32, hh], scalar1=cum[:, kb, hh:hh + 1], scalar2=None,
                                                                             op0=ALU.subtract), reads=[r_ref, r_cum], writes=[r_bias])
            Wv = load_unit(1 + j)
            nb = h_load(0)
            for tb in range(8):
                hb_i = nb
                if tb + 1 < 8:
                    nb = h_load(tb + 1)
                sl = slice(tb * 512, (tb + 1) * 512)
                proj_fm(Wv, 0, 128, hb_i, ps[5], rps[5])
                S.op("act", lambda e, sl=sl: e.activation(out=qT[:, sl], in_=ps[5][:, :], func=AF.Copy, scale=0.125), reads=[rps[5]], writes=[r_q])
                proj_fm(Wv, 128, 128, hb_i, ps[6], rps[6])
                S.op("dve", lambda e, sl=sl: e.tensor_copy(out=kT[:, sl], in_=ps[6][:, :]), reads=[rps[6]], writes=[r_k])
                proj_fm(Wv, 256, 128, hb_i, ps[5], rps[5])
                S.op("act", lambda e, sl=sl: e.activation(out=sgT[:, sl], in_=ps[5][:, :], func=AF.Silu), reads=[rps[5]], writes=[r_sg])
                for tt in range(4):
                    proj_tm(Wv, 384, 128, hb_i, tt, ps[7][:, 128 * tt:128 * tt + 128], rps[7])
                pv = ps[7][:, :].rearrange("p (b d) -> p b d", d=128)
                S.op("act", lambda e, tb=tb, pv=pv: e.activation(out=vaug[:, 4 * tb:4 * tb + 4, 0, 0:64], in_=pv[:, :, 0:64], func=AF.Copy), reads=[rps[7]], writes=[r_v])
                S.op("dve", lambda e, tb=tb, pv=pv: e.tensor_copy(out=vaug[:, 4 * tb:4 * tb + 4, 1, 64:128], in_=pv[:, :, 64:128]), reads=[rps[7]], writes=[r_v])
            cnt = 0
            for Q4 in range(8):
                yb_i = Q4 % 2
                for h in range(2):
                    pr = slice(64 * h, 64 * h + 64)
                    po, rpo = ps[h], rps[h]
                    nkb = 4 * Q4 + 4
                    for kb in range(nkb):
                        qlo = max(512 * Q4, 128 * kb)
                        n = 512 * (Q4 + 1) - qlo
                        c0 = qlo - 512 * Q4
                        sb = 2 + cnt % 3
                        eb = cnt % 3
                        cnt += 1
                        S.op("pe", lambda e, sb=sb, pr=pr, kb=kb, qlo=qlo, n=n: e.matmul(ps[sb][:, 0:n], lhsT=kT[pr, 128 * kb:128 * kb + 128], rhs=qT[pr, qlo:qlo + n], start=True, stop=True),
                             reads=[r_q, r_k], writes=[rps[sb]])
                        for qi in range(n // 128):
                            qb = qlo // 128 + qi
                            S.op("act", lambda e, sb=sb, eb=eb, qi=qi, h=h, kb=kb, qb=qb: e.activation(out=Eb[eb][:, 128 * qi:128 * qi + 128], in_=ps[sb][:, 128 * qi:128 * qi + 128], func=AF.Exp,
                                                                                                     bias=biasT[:, h, kb, qb:qb + 1], scale=1.0),
                                 reads=[rps[sb], r_bias], writes=[r_E[eb]])
                        if kb >= 4 * Q4:
                            S.op("dve", lambda e, eb=eb: e.tensor_tensor(out=Eb[eb][:, 0:128], in0=Eb[eb][:, 0:128], in1=mask_fox, op=ALU.mult), reads=[r_E[eb], r_cb], writes=[r_E[eb]])
                        S.op("pe", lambda e, po=po, c0=c0, n=n, kb=kb, h=h, eb=eb, nkb=nkb: e.matmul(po[:, c0:c0 + n], lhsT=vaug[:, kb, h, :], rhs=Eb[eb][:, 0:n], start=(kb == 0), stop=(kb == nkb - 1)),
                             reads=[r_v, r_E[eb]], writes=[rpo])
                    dn = slice(64, 128) if h == 0 else slice(0, 64)
                    S.op("act", lambda e, po=po, pr=pr, dn=dn: e.activation(out=dsh[pr, :], in_=po[dn, :], func=AF.Copy), reads=[rpo], writes=[r_dsh])
                    S.op("dve", lambda e, pr=pr: e.reciprocal(out=dsh[pr, :], in_=dsh[pr, :]), reads=[r_dsh], writes=[r_dsh])
                    S.op("dve", lambda e, po=po, pr=pr: e.tensor_tensor(out=ytmp[pr, :], in0=po[pr, :], in1=dsh[pr, :], op=ALU.mult), reads=[rpo, r_dsh], writes=[r_ytmp])
                    S.op("dve", lambda e, pr=pr, yb_i=yb_i, Q4=Q4: e.tensor_tensor(out=ybuf[yb_i][pr, :], in0=ytmp[pr, :], in1=sgT[pr, 512 * Q4:512 * Q4 + 512], op=ALU.mult),
                         reads=[r_ytmp, r_sg], writes=[r_yb[yb_i]])
                S.dma("pool", lambda e, yb_i=yb_i, Q4=Q4: e.dma_start(out=yT_d[yr0 + 384 + 128 * j:yr0 + 512 + 128 * j, 512 * Q4:512 * Q4 + 512], in_=ybuf[yb_i]), reads=[r_yb[yb_i]], writes=[r_yTd])
            S.barrier()

        def mem_unit():
            A.off = WORK
            yr0 = 1024 * gsel["g"]
            hm = A.bf16(NKC * 256).rearrange("p (kc t) -> p kc t", t=256); r_hm = Res()
            mkT = A.bf16(512).rearrange("p (d m) -> p d m", m=256); r_mk = Res()
            mv = A.bf16(512).rearrange("p (m d) -> p m d", d=256); r_mv = Res()
            mqT = A.bf16(1024).rearrange("p (d t) -> p d t", t=512); r_mq = Res()
            sgm = A.bf16(1024).rearrange("p (d t) -> p d t", t=512); r_sgm = Res()
            Em = A.bf16(1024).rearrange("p (m t) -> p m t", t=512); r_Em = Res()
            rec = A.f32(512); r_rec = Res()
            ytmp = A.f32(512); r_ytmp = Res()
            ybuf = [A.bf16(512) for _ in range(2)]; r_yb = [Res(), Res()]
            S.dma("sp", lambda e: e.dma_start(out=hm, in_=hmTv), reads=[r_hmTd], writes=[r_hm])
            Wk = load_unit(None, src=wkv4[gsel["g"]], n=512)
            for dt in range(2):
                for kc in range(NKC):
                    S.op("pe", lambda e, dt=dt, kc=kc: e.matmul(ps[0][:, 0:256], lhsT=Wk[:, kc, 128 * dt:128 * dt + 128], rhs=hm[:, kc, :], start=(kc == 0), stop=(kc == NKC - 1)),
                         reads=[r_W, r_hm], writes=[rps[0]])
                S.op("act", lambda e, dt=dt: e.activation(out=mkT[:, dt, :], in_=ps[0][:, 0:256], func=AF.Copy), reads=[rps[0]], writes=[r_mk])
            for mt in range(2):
                for kc in range(NKC):
                    S.op("pe", lambda e, mt=mt, kc=kc: e.matmul(ps[1][:, 0:256], lhsT=hm[:, kc, 128 * mt:128 * mt + 128], rhs=Wk[:, kc, 256:512], start=(kc == 0), stop=(kc == NKC - 1)),
                         reads=[r_W, r_hm], writes=[rps[1]])
                S.op("dve", lambda e, mt=mt: e.tensor_copy(out=mv[:, mt, :], in_=ps[1][:, 0:256]), reads=[rps[1]], writes=[r_mv])
            Wv = load_unit(4)
            nb = h_load(0)
            for tb in range(8):
                hb_i = nb
                if tb + 1 < 8:
                    nb = h_load(tb + 1)
                for dt in range(2):
                    proj_fm(Wv, 128 * dt, 128, hb_i, ps[2 + dt], rps[2 + dt])
                    S.op("act", lambda e, dt=dt: e.activation(out=mqT[:, dt, :], in_=ps[2 + dt][:, :], func=AF.Copy, scale=1.0 / 16.0), reads=[rps[2 + dt]], writes=[r_mq])
                for dt in range(2):
                    proj_fm(Wv, 256 + 128 * dt, 128, hb_i, ps[2 + dt], rps[2 + dt])
                    S.op("act", lambda e, dt=dt: e.activation(out=sgm[:, dt, :], in_=ps[2 + dt][:, :], func=AF.Silu), reads=[rps[2 + dt]], writes=[r_sgm])
                for mt in range(2):
                    for dt in range(2):
                        S.op("pe", lambda e, mt=mt, dt=dt: e.matmul(ps[4][:, :], lhsT=mkT[:, dt, 128 * mt:128 * mt + 128], rhs=mqT[:, dt, :], start=(dt == 0), stop=(dt == 1)),
                             reads=[r_mk, r_mq], writes=[rps[4]])
                    S.op("act", lambda e, mt=mt: e.activation(out=Em[:, mt, :], in_=ps[4][:, :], func=AF.Exp), reads=[rps[4]], writes=[r_Em])
                for mt in range(2):
                    S.op("pe", lambda e, mt=mt: e.matmul(ps[5][:, :], lhsT=ones_b, rhs=Em[:, mt, :], start=(mt == 0), stop=(mt == 1)), reads=[r_Em, r_cb], writes=[rps[5]])
                S.op("dve", lambda e: e.reciprocal(out=rec, in_=ps[5][:, :]), reads=[rps[5]], writes=[r_rec])
                for dt in range(2):
                    for mt in range(2):
                        S.op("pe", lambda e, mt=mt, dt=dt: e.matmul(ps[6 + dt][:, :], lhsT=mv[:, mt, 128 * dt:128 * dt + 128], rhs=Em[:, mt, :], start=(mt == 0), stop=(mt == 1)),
                             reads=[r_mv, r_Em], writes=[rps[6 + dt]])
                    S.op("dve", lambda e, dt=dt: e.tensor_tensor(out=ytmp, in0=ps[6 + dt][:, :], in1=rec, op=ALU.mult), reads=[rps[6 + dt], r_rec], writes=[r_ytmp])
                    S.op("dve", lambda e, dt=dt: e.tensor_tensor(out=ybuf[dt], in0=ytmp, in1=sgm[:, dt, :], op=ALU.mult), reads=[r_ytmp, r_sgm], writes=[r_yb[dt]])
                    S.dma("pool", lambda e, dt=dt, tb=tb: e.dma_start(out=yT_d[yr0 + 768 + 128 * dt:yr0 + 896 + 128 * dt, 512 * tb:512 * tb + 512], in_=ybuf[dt]), reads=[r_yb[dt]], writes=[r_yTd])
            S.barrier()

        def rwkv_unit(j):
            A.off = WORK - NKC * 256
            H = 64
            lo = [[A.bf16(512) for _ in range(2)] for _ in range(2)]; r_lo = [[Res(), Res()], [Res(), Res()]]
            f = lambda n: A.f32(n)
            praw = [[f(513) for _ in range(3)] for _ in range(2)]
            r_praw = [[Res() for _ in range(3)] for _ in range(2)]
            ST = [f(64) for _ in range(2)]; r_ST = [Res(), Res()]
            sgT = A.bf16(512); r_sg = Res()
            rs, ks, vs = f(512), f(512), f(512); r_rs, r_ks, r_vs = Res(), Res(), Res()
            alpha, lw, cumc, cume = f(512), f(512), f(512), f(512)
            r_alpha, r_lw, r_cumc, r_cume = Res(), Res(), Res(), Res()
            Wi, We, Winv = f(512), f(512), f(512); r_Wi, r_We, r_Winv = Res(), Res(), Res()
            kk, t1, km, rkT = f(512), f(512), f(512), f(512); r_kk, r_t1, r_km, r_rk = Res(), Res(), Res(), Res()
            aT, rT, bT, kTt, vT = A.bf16(512), A.bf16(512), A.bf16(512), A.bf16(512), A.bf16(512)
            r_aT, r_rT, r_bT, r_kTt, r_vT = Res(), Res(), Res(), Res(), Res()
            tm = A.bf16(8 * 4 * 64).rearrange("p (c a d) -> p c a d", a=4, d=64); r_tm = Res()
            Pm = A.bf16(512).rearrange("p (q m t) -> p q m t", q=2, m=4); r_Pm = Res()
            Dk = [A.bf16(256).rearrange("p (q s t) -> p q s t", q=2, s=2) for _ in range(2)]; r_Dk = [Res(), Res()]
            Xf = f(256).rearrange("p (q d) -> p q d", q=2); r_Xf = Res()
            Xb = A.bf16(256).rearrange("p (q d) -> p q d", q=2); r_Xb = Res()
            RhT = f(128).rearrange("p (q t) -> p q t", q=2); r_Rh = Res()
            GpT = f(128).rearrange("p (q t) -> p q t", q=2); r_Gp = Res()
            HmW = f(128).rearrange("p (q t) -> p q t", q=2); r_Hm = Res()
            Y0 = f(128).rearrange("p (q t) -> p q t", q=2); r_Y0 = Res()
            Ytok = f(512).rearrange("p (c d) -> p c d", d=64); r_Yt = Res()
            yc = f(512).rearrange("p (c d) -> p c d", d=64); r_yc = Res()
            ysq = f(512).rearrange("p (c d) -> p c d", d=64); r_ysq = Res()
            st1, st2, bon = f(8), f(8), f(8); r_st1, r_st2, r_bon = Res(), Res(), Res()
            ybuf = [A.bf16(512) for _ in range(2)]; r_yb = [Res(), Res()]
            assert A.off <= ARENA_N, A.off
            for h in range(2):
                for a in range(3):
                    S.op("dve", lambda e, h=h, a=a: e.memset(praw[h][a][:, 0:1], 0.0), writes=[r_praw[h][a]])
                S.op("dve", lambda e, h=h: e.memset(ST[h][0:H, :], 0.0), writes=[r_ST[h]])
            Wv = load_unit(5 + j)
            for tb in range(8):
                hcount[0] = 0
                hb_i = h_load(tb)
                t0 = tb * 512
                lb = tb % 2
                for w_ in range(2):
                    S.dma("sp", lambda e, w_=w_, lb=lb, t0=t0: e.dma_start(out=lo[w_][lb], in_=lora_d[w_, :, t0:t0 + 512]), reads=[r_lorad], writes=[r_lo[w_][lb]])
                for h in range(2):
                    hh = 2 * j + h
                    pc = 66 + 8 * hh
                    PV = [p_f[0:H, pc + k_:pc + k_ + 1] for k_ in range(8)]
                    P = (lambda PV: (lambda k: PV[k]))(PV)
                    cw = 64 * hh
                    for a in range(3):
                        pb = a % 2
                        proj_fm(Wv, 256 * h + 64 * a, 64, hb_i, ps[pb], rps[pb])
                        if a % 2 == 0:
                            S.op("act", lambda e, h=h, a=a, pb=pb: e.activation(out=praw[h][a][0:H, 1:513], in_=ps[pb][0:H, :], func=AF.Copy), reads=[rps[pb]], writes=[r_praw[h][a]])
                        else:
                            S.op("dve", lambda e, h=h, a=a, pb=pb: e.tensor_copy(out=praw[h][a][0:H, 1:513], in_=ps[pb][0:H, :]), reads=[rps[pb]], writes=[r_praw[h][a]])
                    proj_fm(Wv, 256 * h + 192, 64, hb_i, ps[1], rps[1])
                    S.op("act", lambda e: e.activation(out=sgT[0:H, :], in_=ps[1][0:H, :], func=AF.Silu), reads=[rps[1]], writes=[r_sg])
                    shift_block(praw[h][0], r_praw[h][0], P(0), H, rs[0:H, :], r_rs)
                    shift_block(praw[h][1], r_praw[h][1], P(1), H, ks[0:H, :], r_ks)
                    shift_block(praw[h][2], r_praw[h][2], P(2), H, vs[0:H, :], r_vs)
                    S.op("pe", lambda e, cw=cw, lb=lb: e.matmul(ps[0][0:H, :], lhsT=wl_b[:, cw:cw + 64], rhs=lo[0][lb], start=True, stop=True),
                         reads=[r_wlb, r_lo[0][lb]], writes=[rps[0]])
                    S.op("act", lambda e, P=P: e.activation(out=lw[0:H, :], in_=ps[0][0:H, :], func=AF.Sigmoid, bias=P(3), scale=1.0), reads=[rps[0], r_pf], writes=[r_lw])
                    S.op("pe", lambda e, cw=cw, lb=lb: e.matmul(ps[1][0:H, :], lhsT=wl_b[:, 384 + cw:384 + cw + 64], rhs=lo[1][lb], start=True, stop=True),
                         reads=[r_wlb, r_lo[1][lb]], writes=[rps[1]])
                    S.op("act", lambda e, P=P: e.activation(out=alpha[0:H, :], in_=ps[1][0:H, :], func=AF.Sigmoid, bias=P(4), scale=1.0), reads=[rps[1], r_pf], writes=[r_alpha])
                    S.op("dve", lambda e: e.tensor_scalar(out=lw[0:H, :], in0=lw[0:H, :], scalar1=-EXPM05, scalar2=None, op0=ALU.mult), reads=[r_lw], writes=[r_lw])
                    S.op("dve", lambda e: e.tensor_tensor_scan(out=cumc[0:H, :], data0=rmask[0:H, :], data1=lw[0:H, :], initial=0.0, op0=ALU.mult, op1=ALU.add),
                         reads=[r_lw, r_cf], writes=[r_cumc])
                    S.op("dve", lambda e: e.tensor_tensor(out=cume[0:H, :], in0=cumc[0:H, :], in1=lw[0:H, :], op=ALU.subtract), reads=[r_cumc, r_lw], writes=[r_cume])
                    S.op("act", lambda e: e.activation(out=Wi[0:H, :], in_=cumc[0:H, :], func=AF.Exp), reads=[r_cumc], writes=[r_Wi])
                    S.op("act", lambda e: e.activation(out=We[0:H, :], in_=cume[0:H, :], func=AF.Exp), reads=[r_cume], writes=[r_We])
                    S.op("act", lambda e: e.activation(out=Winv[0:H, :], in_=cumc[0:H, :], func=AF.Exp, scale=-1.0), reads=[r_cumc], writes=[r_Winv])
                    S.op("dve", lambda e, P=P: e.tensor_scalar(out=kk[0:H, :], in0=ks[0:H, :], scalar1=P(5), scalar2=None, op0=ALU.mult), reads=[r_ks, r_pf], writes=[r_kk])
                    S.op("act", lambda e: e.activation(out=t1[0:H, :], in_=kk[0:H, :], func=AF.Square), reads=[r_kk], writes=[r_t1])
                    S.op("pe", lambda e: e.matmul(ps[0][0:H, :], lhsT=ones_f[0:H, 0:H], rhs=t1[0:H, :], start=True, stop=True), reads=[r_t1, r_cf], writes=[rps[0]])
                    S.op("act", lambda e: e.activation(out=t1[0:H, :], in_=ps[0][0:H, :], func=AF.Sqrt), reads=[rps[0]], writes=[r_t1])
                    S.op("dve", lambda e: e.tensor_scalar(out=t1[0:H, :], in0=t1[0:H, :], scalar1=1e-12, scalar2=None, op0=ALU.max), reads=[r_t1], writes=[r_t1])
                    S.op("dve", lambda e: e.reciprocal(out=t1[0:H, :], in_=t1[0:H, :]), reads=[r_t1], writes=[r_t1])
                    S.op("dve", lambda e: e.tensor_tensor(out=kk[0:H, :], in0=kk[0:H, :], in1=t1[0:H, :], op=ALU.mult), reads=[r_kk, r_t1], writes=[r_kk])
                    S.op("dve", lambda e: e.scalar_tensor_tensor(out=aT[0:H, :], in0=kk[0:H, :], scalar=-1.0, in1=We[0:H, :], op0=ALU.mult, op1=ALU.mult), reads=[r_kk, r_We], writes=[r_aT])
                    S.op("dve", lambda e: e.tensor_tensor(out=t1[0:H, :], in0=kk[0:H, :], in1=alpha[0:H, :], op=ALU.mult), reads=[r_kk, r_alpha], writes=[r_t1])
                    S.op("dve", lambda e: e.tensor_tensor(out=bT[0:H, :], in0=t1[0:H, :], in1=Winv[0:H, :], op=ALU.mult), reads=[r_t1, r_Winv], writes=[r_bT])
                    S.op("dve", lambda e: e.tensor_tensor(out=rT[0:H, :], in0=rs[0:H, :], in1=Wi[0:H, :], op=ALU.mult), reads=[r_rs, r_Wi], writes=[r_rT])
                    S.op("dve", lambda e, P=P: e.tensor_scalar(out=t1[0:H, :], in0=alpha[0:H, :], scalar1=-1.0, scalar2=P(6), op0=ALU.add, op1=ALU.mult), reads=[r_alpha, r_pf], writes=[r_t1])
                    S.op("dve", lambda e: e.scalar_tensor_tensor(out=km[0:H, :], in0=t1[0:H, :], scalar=1.0, in1=ks[0:H, :], op0=ALU.add, op1=ALU.mult), reads=[r_t1, r_ks], writes=[r_km])
                    S.op("dve", lambda e: e.tensor_tensor(out=kTt[0:H, :], in0=km[0:H, :], in1=Winv[0:H, :], op=ALU.mult), reads=[r_km, r_Winv], writes=[r_kTt])
                    S.op("dve", lambda e, P=P: e.scalar_tensor_tensor(out=rkT[0:H, :], in0=rs[0:H, :], scalar=P(7), in1=km[0:H, :], op0=ALU.mult, op1=ALU.mult), reads=[r_rs, r_km, r_pf], writes=[r_rk])
                    S.op("act", lambda e: e.activation(out=vT[0:H, :], in_=vs[0:H, :], func=AF.Copy), reads=[r_vs], writes=[r_vT])
                    for half in range(2):
                        pbv = ps[2][:, :].bitcast(BF16).rearrange("p (c a d) -> p c a d", a=4, d=64)
                        for c4 in range(4):
                            c = 4 * half + c4
                            for ai, (arr, rr) in enumerate(((aT, r_aT), (bT, r_bT), (kTt, r_kTt), (vT, r_vT))):
                                S.op("pe", lambda e, c4=c4, c=c, ai=ai, arr=arr, pbv=pbv: e.transpose(out=pbv[0:H, c4, ai, :], in_=arr[0:H, 64 * c:64 * c + 64], identity=ident_b[0:H, 0:H]),
                                     reads=[rr, r_cb], writes=[rps[2]])
                        S.op("act", lambda e, half=half, pbv=pbv: e.activation(out=tm[0:H, 4 * half:4 * half + 4, :, :], in_=pbv[0:H, :, :, :], func=AF.Copy), reads=[rps[2]], writes=[r_tm])
                    for du in range(4):
                        cs = (2 * du, 2 * du + 1)
                        cl = (lambda cs: (lambda q: slice(64 * cs[q], 64 * cs[q] + 64)))(cs)
                        pP = ps[3][0:H, :].rearrange("p (q m t) -> p q m t", q=2, m=4)
                        for q in range(2):
                            for m, (l_, r_, rl, rr) in enumerate(((bT, aT, r_bT, r_aT), (kTt, aT, r_kTt, r_aT), (bT, rT, r_bT, r_rT), (kTt, rT, r_kTt, r_rT))):
                                S.op("pe", lambda e, q=q, m=m, l_=l_, r_=r_, pP=pP, cl=cl: e.matmul(pP[:, q, m, :], lhsT=l_[0:H, cl(q)], rhs=r_[0:H, cl(q)], start=True, stop=True),
                                     reads=[rl, rr], writes=[rps[3]])
                        S.op("dve", lambda e: e.tensor_tensor(out=Pm[0:H].rearrange("p q m t -> p (q m t)"), in0=ps[3][0:H, :], in1=maskP[0:H, :], op=ALU.mult), reads=[rps[3], r_cb], writes=[r_Pm])
                        pD = ps[4][0:H, 0:256].rearrange("p (q s t) -> p q s t", q=2, s=2)
                        for q in range(2):
                            S.op("pe", lambda e, q=q, pD=pD, cl=cl: e.matmul(pD[:, q, 0, :], lhsT=aT[0:H, cl(q)], rhs=bT[0:H, cl(q)], start=True, stop=True), reads=[r_aT, r_bT], writes=[rps[4]])
                        S.op("dve", lambda e, pD=pD: e.tensor_tensor(out=Dk[0][0:H, :, 0, :], in0=pD[:, :, 0, :], in1=maskN0[0:H, :].rearrange("p (q t) -> p q t", q=2), op=ALU.mult),
                             reads=[rps[4], r_cb], writes=[r_Dk[0]])
                        S.op("act", lambda e: e.activation(out=Dk[0][0:H, :, 1, :], in_=Pm[0:H, :, 0, :], func=AF.Copy), reads=[r_Pm], writes=[r_Dk[0]])
                        pX = ps[5][0:H, 0:256].rearrange("p (q d) -> p q d", q=2)
                        for q in range(2):
                            S.op("pe", lambda e, q=q, pX=pX, cs=cs: e.matmul(pX[:, q, 0:64], lhsT=Pm[0:H, q, 1, :], rhs=tm[0:H, cs[q], 3, :], start=True, stop=True), reads=[r_Pm, r_tm], writes=[rps[5]])
                        S.op("act", lambda e, pX=pX: e.activation(out=Xf[0:H, :, 0:64], in_=pX[:, :, 0:64], func=AF.Copy), reads=[rps[5]], writes=[r_Xf])
                        S.op("dve", lambda e, cs=cs: e.tensor_copy(out=Xf[0:H, :, 64:128], in_=tm[0:H, cs[0]:cs[1] + 1, 0, :]), reads=[r_tm], writes=[r_Xf])
                        S.op("act", lambda e: e.activation(out=Xb[0:H], in_=Xf[0:H], func=AF.Copy), reads=[r_Xf], writes=[r_Xb])
                        for lv in range(6):
                            cur = lv % 2
                            nxt = 1 - cur
                            for q in range(2):
                                S.op("pe", lambda e, q=q, cur=cur, pX=pX: e.matmul(pX[:, q, :], lhsT=Dk[cur][0:H, q, 1, :], rhs=Xb[0:H, q, :], start=True, stop=True),
                                     reads=[r_Dk[cur], r_Xb], writes=[rps[5]])
                            S.op("dve", lambda e, pX=pX: e.tensor_tensor(out=Xf[0:H], in0=Xf[0:H], in1=pX, op=ALU.add), reads=[rps[5], r_Xf], writes=[r_Xf])
                            S.op("act", lambda e: e.activation(out=Xb[0:H], in_=Xf[0:H], func=AF.Copy), reads=[r_Xf], writes=[r_Xb])
                            if lv < 5:
                                for q in range(2):
                                    S.op("pe", lambda e, q=q, cur=cur, pD=pD: e.matmul(pD[:, q, 0, :], lhsT=Dk[cur][0:H, q, 1, :], rhs=Dk[cur][0:H, q, 0, :], start=True, stop=True),
                                         reads=[r_Dk[cur]], writes=[rps[4]])
                                    S.op("pe", lambda e, q=q, cur=cur, pD=pD: e.matmul(pD[:, q, 1, :], lhsT=Dk[cur][0:H, q, 0, :], rhs=Dk[cur][0:H, q, 1, :], start=True, stop=True),
                                         reads=[r_Dk[cur]], writes=[rps[4]])
                                S.op("act", lambda e, nxt=nxt: e.activation(out=Dk[nxt][0:H].rearrange("p q s t -> p (q s t)"), in_=ps[4][0:H, 0:256], func=AF.Copy), reads=[rps[4]], writes=[r_Dk[nxt]])
                        pR = ps[6][0:H, :].rearrange("p (k q t) -> p k q t", k=4, q=2)
                        for q in range(2):
                            c = cs[q]
                            S.op("pe", lambda e, q=q, pR=pR: e.matmul(pR[:, 0, q, :], lhsT=Xb[0:H, q, 64:128], rhs=Pm[0:H, q, 2, :], start=True, stop=True), reads=[r_Xb, r_Pm], writes=[rps[6]])
                            S.op("pe", lambda e, q=q, c=c, pR=pR: e.matmul(pR[:, 1, q, :], lhsT=Xb[0:H, q, 64:128], rhs=tm[0:H, c, 1, :], start=True, stop=True), reads=[r_Xb, r_tm], writes=[rps[6]])
                            S.op("pe", lambda e, q=q, c=c, pR=pR: e.matmul(pR[:, 2, q, :], lhsT=tm[0:H, c, 1, :], rhs=Xb[0:H, q, 0:64], start=True, stop=False), reads=[r_Xb, r_tm], writes=[rps[6]])
                            S.op("pe", lambda e, q=q, c=c, pR=pR: e.matmul(pR[:, 2, q, :], lhsT=tm[0:H, c, 2, :], rhs=tm[0:H, c, 3, :], start=False, stop=True), reads=[r_tm], writes=[rps[6]])
                            S.op("pe", lambda e, q=q, pR=pR: e.matmul(pR[:, 3, q, :], lhsT=Pm[0:H, q, 2, :], rhs=Xb[0:H, q, 0:64], start=True, stop=False), reads=[r_Xb, r_Pm], writes=[rps[6]])
                            S.op("pe", lambda e, q=q, c=c, pR=pR: e.matmul(pR[:, 3, q, :], lhsT=Pm[0:H, q, 3, :], rhs=tm[0:H, c, 3, :], start=False, stop=True), reads=[r_Pm, r_tm], writes=[rps[6]])
                        S.op("dve", lambda e, pR=pR, cs=cs: e.tensor_tensor(out=RhT[0:H], in0=pR[:, 0, :, :], in1=rT[0:H, 64 * cs[0]:64 * cs[0] + 128].rearrange("p (q t) -> p q t", q=2), op=ALU.add),
                             reads=[rps[6], r_rT], writes=[r_Rh])
                        S.op("dve", lambda e, pR=pR: e.tensor_tensor(out=GpT[0:H], in0=pR[:, 1, :, :], in1=ident_f[0:H, 0:H].unsqueeze(1).to_broadcast([H, 2, 64]), op=ALU.add),
                             reads=[rps[6], r_cf], writes=[r_Gp])
                        for q in range(2):
                            ce = 64 * cs[q] + 63
                            S.op("dve", lambda e, q=q, ce=ce, pR=pR: e.tensor_scalar(out=HmW[0:H, q, :], in0=pR[:, 2, q, :], scalar1=Wi[0:H, ce:ce + 1], scalar2=None, op0=ALU.mult),
                                 reads=[rps[6], r_Wi], writes=[r_Hm])
                        S.op("act", lambda e, pR=pR: e.activation(out=Y0[0:H], in_=pR[:, 3, :, :], func=AF.Copy), reads=[rps[6]], writes=[r_Y0])
                        for q in range(2):
                            c = cs[q]
                            ce = 64 * c + 63
                            S.op("pe", lambda e, q=q, h=h: e.matmul(ps[7][0:H, 0:64], lhsT=RhT[0:H, q, :], rhs=ST[h][0:H, :], start=True, stop=True), reads=[r_Rh, r_ST[h]], writes=[rps[7]])
                            S.op("pe", lambda e, q=q, h=h: e.matmul(ps[7][0:H, 64:128], lhsT=GpT[0:H, q, :], rhs=ST[h][0:H, :], start=True, stop=True), reads=[r_Gp, r_ST[h]], writes=[rps[7]])
                            S.op("dve", lambda e, q=q, c=c: e.tensor_tensor(out=Ytok[0:H, c, :], in0=ps[7][0:H, 0:64], in1=Y0[0:H, q, :], op=ALU.add), reads=[rps[7], r_Y0], writes=[r_Yt])
                            S.op("dve", lambda e, q=q, h=h, ce=ce: e.scalar_tensor_tensor(out=ST[h][0:H, :], in0=ps[7][0:H, 64:128], scalar=Wi[0:H, ce:ce + 1], in1=HmW[0:H, q, :], op0=ALU.mult, op1=ALU.add),
                                 reads=[rps[7], r_Wi, r_Hm], writes=[r_ST[h]])
                    S.op("dve", lambda e: e.tensor_reduce(out=st1[0:H, :], in_=Ytok[0:H], axis=AX.X, op=ALU.add), reads=[r_Yt], writes=[r_st1])
                    S.op("dve", lambda e: e.tensor_scalar(out=st1[0:H, :], in0=st1[0:H, :], scalar1=1.0 / 64.0, scalar2=None, op0=ALU.mult), reads=[r_st1], writes=[r_st1])
                    S.op("dve", lambda e: e.tensor_tensor(out=yc[0:H], in0=Ytok[0:H], in1=st1[0:H, :].unsqueeze(2).to_broadcast([H, 8, 64]), op=ALU.subtract), reads=[r_Yt, r_st1], writes=[r_yc])
                    S.op("act", lambda e: e.activation(out=ysq[0:H], in_=yc[0:H], func=AF.Square), reads=[r_yc], writes=[r_ysq])
                    S.op("dve", lambda e: e.tensor_reduce(out=st2[0:H, :], in_=ysq[0:H], axis=AX.X, op=ALU.add), reads=[r_ysq], writes=[r_st2])
                    S.op("dve", lambda e: e.tensor_scalar(out=st2[0:H, :], in0=st2[0:H, :], scalar1=1.0 / 64.0, scalar2=64e-5, op0=ALU.mult, op1=ALU.add), reads=[r_st2], writes=[r_st2])
                    S.op("act", lambda e: e.activation(out=st2[0:H, :], in_=st2[0:H, :], func=AF.Sqrt), reads=[r_st2], writes=[r_st2])
                    S.op("dve", lambda e: e.reciprocal(out=st2[0:H, :], in_=st2[0:H, :]), reads=[r_st2], writes=[r_st2])
                    S.op("dve", lambda e: e.tensor_tensor(out=yc[0:H], in0=yc[0:H], in1=st2[0:H, :].unsqueeze(2).to_broadcast([H, 8, 64]), op=ALU.mult), reads=[r_yc, r_st2], writes=[r_yc])
                    S.op("dve", lambda e, cw=cw: e.tensor_tensor(out=yc[0:H], in0=yc[0:H], in1=lnw_bc[0:H, cw:cw + 64].unsqueeze(1).to_broadcast([H, 8, 64]), op=ALU.mult), reads=[r_yc, r_rowb], writes=[r_yc])
                    S.op("dve", lambda e, cw=cw: e.tensor_tensor(out=yc[0:H], in0=yc[0:H], in1=lnb_bc[0:H, cw:cw + 64].unsqueeze(1).to_broadcast([H, 8, 64]), op=ALU.add), reads=[r_yc, r_rowb], writes=[r_yc])
                    for c in range(8):
                        S.op("pe", lambda e, c=c: e.matmul(ps[0][0:H, c:c + 1], lhsT=rkT[0:H, 64 * c:64 * c + 64], rhs=ones_f[0:H, 0:1], start=True, stop=True), reads=[r_rk, r_cf], writes=[rps[0]])
                    S.op("act", lambda e: e.activation(out=bon[0:H, :], in_=ps[0][0:H, 0:8], func=AF.Copy), reads=[rps[0]], writes=[r_bon])
                    S.op("dve", lambda e: e.tensor_tensor(out=ysq[0:H], in0=tm[0:H, :, 3, :], in1=bon[0:H, :].unsqueeze(2).to_broadcast([H, 8, 64]), op=ALU.mult), reads=[r_tm, r_bon], writes=[r_ysq])
                    S.op("dve", lambda e: e.tensor_tensor(out=yc[0:H], in0=yc[0:H], in1=ysq[0:H], op=ALU.add), reads=[r_yc, r_ysq], writes=[r_yc])
                    if DBG and os.environ.get("RWDBG") == "%d,%d,%d" % (j, tb, h):
                        for k_, (buf_, rr_) in enumerate(((rs, r_rs), (ks, r_ks), (vs, r_vs), (lw, r_lw), (alpha, r_alpha), (kk, r_kk), (Ytok.rearrange("p c d -> p (c d)"), r_Yt), (yc.rearrange("p c d -> p (c d)"), r_yc))):
                            S.dma("sp", lambda e, k_=k_, buf_=buf_: e.dma_start(out=dbg[0:H, 512 * k_:512 * k_ + 512], in_=buf_[0:H, :]), reads=[rr_])
                        ysq2 = ysq.rearrange("p c d -> p (c d)")
                        for k_, src_ in enumerate((lo[0][lb][0:64, :], lo[1][lb][0:64, :], wl_b[0:64, cw:cw + 64])):
                            n_ = 512 if k_ < 2 else 64
                            S.op("act", lambda e, src_=src_, n_=n_: e.activation(out=ysq2[0:64, 0:n_], in_=src_, func=AF.Copy), reads=[r_lo[0][lb], r_lo[1][lb], r_wlb], writes=[r_ysq])
                            S.dma("sp", lambda e, k_=k_, n_=n_: e.dma_start(out=dbg[64:128, 512 * k_:512 * k_ + n_], in_=ysq2[0:64, 0:n_]), reads=[r_ysq])
                        S.barrier()
                    for c in range(8):
                        S.op("pe", lambda e, c=c: e.transpose(out=ps[1][0:H, 64 * c:64 * c + 64], in_=yc[0:H, c, :], identity=ident_f[0:H, 0:H]), reads=[r_yc, r_cf], writes=[rps[1]])
                    yi = (2 * tb + h) % 2
                    S.op("dve", lambda e, yi=yi: e.tensor_tensor(out=ybuf[yi][0:H, :], in0=ps[1][0:H, :], in1=sgT[0:H, :], op=ALU.mult), reads=[rps[1], r_sg], writes=[r_yb[yi]])
                    row0 = 1024 * gsel["g"] + 128 * j + 64 * h
                    S.dma("pool", lambda e, yi=yi, row0=row0, t0=t0: e.dma_start(out=yT_d[row0:row0 + 64, t0:t0 + 512], in_=ybuf[yi][0:H, :]), reads=[r_yb[yi]], writes=[r_yTd])
            S.barrier()

        for g in range(G):
            if g > 0:
                S.barrier()
                load_params(g)
            unit0()
            if stage >= 1:
                for j in range(3):
                    fox_unit(j)
            if stage >= 2:
                mem_unit()
            if stage >= 3:
                for j in range(3):
                    rwkv_unit(j)
        if FUSED:
            yv = yT_d.rearrange("(kc p) t -> p kc t", p=128)
            for qtr in range(4):
                S.barrier()
                A.off = PERS
                out_phase(nc, S, A, ps, rps, (lambda qt, qtr=qtr: yv[:, :, 1024 * qtr + 256 * qt:1024 * qtr + 256 * qt + 256]), r_yTd, wout,
                          xfull[1024 * qtr:1024 * qtr + 1024, :], gpost, out[1024 * qtr:1024 * qtr + 1024, :])
        S.barrier()

        _replay(nc, S, st)
    return nc


def _unit_cols(g):
    u0 = list(range(4608, 4864)) + [11008 + 6 * g + i for i in range(6)]
    units = [u0]
    for j in range(3):
        c0 = 384 * g + 128 * j
        units.append(list(range(6400 + c0, 6400 + c0 + 128)) + list(range(7936 + c0, 7936 + c0 + 128))
                     + list(range(11032 + c0, 11032 + c0 + 128)) + list(range(9472 + c0, 9472 + c0 + 128)))
    units.append(list(range(12568 + 256 * g, 12568 + 256 * g + 256)) + list(range(13592 + 256 * g, 13592 + 256 * g + 256)))
    for j in range(3):
        cols = []
        for hh in range(2):
            c0 = 384 * g + 128 * j + 64 * hh
            cols += list(range(c0, c0 + 64)) + list(range(1536 + c0, 1536 + c0 + 64)) + list(range(3072 + c0, 3072 + c0 + 64)) + list(range(4864 + c0, 4864 + c0 + 64))
        units.append(cols)
    return units


def _kc_layout(w):
    n = w.shape[1]
    return np.ascontiguousarray(w.reshape(NKC, 128, n).transpose(1, 0, 2).reshape(128, NKC * n))


def _consts():
    cf = np.zeros((128, 896), np.float32)
    cf[:, 0:128] = np.eye(128)
    sp = np.arange(128)
    cf[:, 128:256] = (sp[:, None] <= sp[None, :])
    cf[:, 256:384] = 1.0
    rm = np.ones(512, np.float32)
    rm[::64] = 0.0
    cf[:, 384:896] = rm[None, :]
    cb = np.zeros((128, 1024), np.float32)
    cb[:, 0:128] = np.eye(128)
    cb[:, 128:256] = 1.0
    cb[:, 256:384] = (sp[None, :] >= sp[:, None])
    j = (sp % 64)[:, None]
    t = np.arange(64)[None, :]
    strict = (t > j).astype(np.float32)
    incl = (t >= j).astype(np.float32)
    mp = np.stack([strict, strict, incl, incl], axis=1)
    cb[:, 384:896] = np.concatenate([mp, mp], axis=1).reshape(128, 512)
    n0 = (t < j).astype(np.float32)
    cb[:, 896:1024] = np.concatenate([n0, n0], axis=1)
    return cf, cb


def prep_inputs(inp):
    f = lambda a: np.ascontiguousarray(np.asarray(a, dtype=np.float32))
    x = f(inp["x"]); mem = f(inp["mem"])
    w_in = f(inp["w_in"])[0]
    cf, cb = _consts()
    mu = f(inp["mu_rwkv"])[0]
    w0 = f(inp["w0"])[0]; a0 = f(inp["a0"])[0]; k_k = f(inp["k_k"])[0]; k_a = f(inp["k_a"])[0]
    r_k = f(inp["r_k"])[0].reshape(-1)
    lnw = f(inp["ln_x_w"])[0]; lnb = f(inp["ln_x_b"])[0]; b_f = f(inp["b_f"])[0]
    g_pre = f(inp["g_pre"])[0]; g_mem = f(inp["g_mem"])[0]; g_post = f(inp["g_post"])[0]
    wdu = f(inp["w_decay_up"])[0]; wiu = f(inp["w_iclr_up"])[0]
    wkv_full = f(inp["w_mem_kv"])[0]; w_out = f(inp["w_out"])[0]
    xT = [np.ascontiguousarray(x[b].T) for b in range(2)]
    memT = [np.ascontiguousarray(mem[b].T) for b in range(2)]
    per_g = []
    for g in range(4):
        units = _unit_cols(g)
        wc = np.concatenate([_kc_layout(w_in[:, u]) for u in units], axis=1)
        wkv = _kc_layout(np.concatenate([wkv_full[:, 256 * g:256 * g + 256], wkv_full[:, 1024 + 256 * g:1024 + 256 * g + 256]], axis=1))
        prm = np.zeros((128, 114), np.float32)
        prm[:, 0:32] = g_pre.reshape(NKC, 128).T
        prm[:, 32:64] = g_mem.reshape(NKC, 128).T
        prm[:, 64] = mu[4608:4736]
        prm[:, 65] = mu[4736:4864]
        for hh in range(6):
            c = 384 * g + 64 * hh
            base = 66 + 8 * hh
            for k, v in enumerate((mu[c:c + 64], mu[1536 + c:1536 + c + 64], mu[3072 + c:3072 + c + 64], w0[c:c + 64], a0[c:c + 64], k_k[c:c + 64], k_a[c:c + 64], r_k[c:c + 64])):
                prm[0:64, base + k] = v
        rowp = np.concatenate([lnw[384 * g:384 * g + 384], lnb[384 * g:384 * g + 384], b_f[6 * g:6 * g + 6]])[None, :].astype(np.float32)
        wlora = np.ascontiguousarray(np.concatenate([wdu[:, 384 * g:384 * g + 384], wiu[:, 384 * g:384 * g + 384]], axis=1))
        per_g.append(dict(wc=wc, wkv=wkv, prm=prm, rowp=np.ascontiguousarray(rowp), wlora=wlora))
    perm = []
    for g in range(4):
        perm += list(range(384 * g, 384 * g + 384)) + list(range(1536 + 384 * g, 1536 + 384 * g + 384)) + list(range(3072 + 256 * g, 3072 + 256 * g + 256))
    wo_p = w_out[np.array(perm), :]
    wout = np.stack([_kc_layout(wo_p[:, 512 * cb_:512 * cb_ + 512]) for cb_ in range(8)], axis=0)
    in_maps = []
    for c in range(8):
        b, g = c // 4, c % 4
        m = dict(xT=xT[b], xrows=np.ascontiguousarray(x[b, 1024 * g:1024 * g + 1024, :]), memT=memT[b], cstf=cf, cstb=cb,
                 gpost=np.ascontiguousarray(g_post[None, :]), wout=wout)
        m.update(per_g[g])
        in_maps.append(m)
    return in_maps


def _replay(nc, S, st):
    block = st.enter_context(nc.Block())
    names = {"pe": "tensor", "act": "scalar", "dve": "vector", "pool": "gpsimd", "sp": "sync"}
    for e in ENGS:
        ops = S.ops[e]
        if not ops:
            continue

        def run(en, ops=ops):
            for o in ops:
                o(en)
        getattr(block, names[e])(run)


def out_phase(nc, S, A, ps, rps, ysrc_of_quarter, r_ysrc, wout, xrows, gpost, out):
    Wo = [A.bf16(NKC * 512).rearrange("p (kc n) -> p kc n", n=512) for _ in range(2)]
    r_Wo = [Res(), Res()]
    ytb = A.bf16(NKC * 256).rearrange("p (kc t) -> p kc t", t=256); r_ytb = [Res(), Res()]
    yo = A.f32(2 * DM).rearrange("p (i n) -> p i n", n=DM); r_yo = Res()
    gpb = A.f32(DM); r_gpb = Res()
    xr = [A.f32(2048) for _ in range(2)]; r_xr = [Res(), Res()]
    ssq = A.f32(16); r_ssq = Res()
    rst = A.f32(2); r_rst = Res()
    junk = A.f32(512); r_junk = Res()
    S.dma("sp", lambda e: e.dma_start(out=gpb, in_=gpost[0:1, :].to_broadcast([128, DM])), writes=[r_gpb])
    wcount = 0
    xcount = 0
    for qt in range(4):
        ysrc = ysrc_of_quarter(qt)
        for hf in range(2):
            S.dma("sp", lambda e, hf=hf, ysrc=ysrc: e.dma_start(out=ytb[:, 16 * hf:16 * hf + 16, :], in_=ysrc[:, 16 * hf:16 * hf + 16, :]), reads=[r_ysrc], writes=[r_ytb[hf]])
        for cb in range(8):
            wb = wcount % 2
            wcount += 1
            for o in range(0, NKC * 512, 2048):
                S.dma("pool", lambda e, wb=wb, cb=cb, o=o: e.dma_start(out=Wo[wb].rearrange("p kc n -> p (kc n)")[:, o:o + 2048], in_=wout[cb, :, o:o + 2048]), writes=[r_Wo[wb]])
            for i in range(2):
                pb = (cb * 2 + i) % 4
                for kc in range(NKC):
                    S.op("pe", lambda e, wb=wb, i=i, kc=kc, pb=pb: e.matmul(ps[pb][:, :], lhsT=ytb[:, kc, 128 * i:128 * i + 128], rhs=Wo[wb][:, kc, :], start=(kc == 0), stop=(kc == NKC - 1)),
                         reads=[r_Wo[wb], r_ytb[kc // 16]], writes=[rps[pb]])
                S.op("dve", lambda e, i=i, cb=cb, pb=pb: e.tensor_copy(out=yo[:, i, 512 * cb:512 * cb + 512], in_=ps[pb][:, :]), reads=[rps[pb]], writes=[r_yo])
                S.op("act", lambda e, pb=pb: e.activation(out=junk, in_=ps[pb][:, :], func=AF.Square), reads=[rps[pb]], writes=[r_junk])
                S.op("dve", lambda e, i=i, cb=cb: e.tensor_reduce(out=ssq[:, 8 * i + cb:8 * i + cb + 1], in_=junk, axis=AX.X, op=ALU.add), reads=[r_junk], writes=[r_ssq])
        for i in range(2):
            S.op("dve", lambda e, i=i: e.tensor_reduce(out=rst[:, i:i + 1], in_=ssq[:, 8 * i:8 * i + 8], axis=AX.X, op=ALU.add), reads=[r_ssq], writes=[r_rst])
            S.op("dve", lambda e, i=i: e.tensor_scalar(out=rst[:, i:i + 1], in0=rst[:, i:i + 1], scalar1=1.0 / DM, scalar2=1e-6, op0=ALU.mult, op1=ALU.add), reads=[r_rst], writes=[r_rst])
            S.op("act", lambda e, i=i: e.activation(out=rst[:, i:i + 1], in_=rst[:, i:i + 1], func=AF.Sqrt), reads=[r_rst], writes=[r_rst])
            S.op("dve", lambda e, i=i: e.reciprocal(out=rst[:, i:i + 1], in_=rst[:, i:i + 1]), reads=[r_rst], writes=[r_rst])
            for hc in range(2):
                xi = xcount % 2
                xcount += 1
                r0 = 256 * qt + 128 * i
                cs_ = slice(2048 * hc, 2048 * hc + 2048)
                S.dma("sp", lambda e, xi=xi, r0=r0, cs_=cs_: e.dma_start(out=xr[xi], in_=xrows[r0:r0 + 128, cs_]), writes=[r_xr[xi]])
                S.op("dve", lambda e, i=i, cs_=cs_: e.scalar_tensor_tensor(out=yo[:, i, cs_], in0=yo[:, i, cs_], scalar=rst[:, i:i + 1], in1=gpb[:, cs_], op0=ALU.mult, op1=ALU.mult),
                     reads=[r_yo, r_rst, r_gpb], writes=[r_yo])
                S.op("dve", lambda e, i=i, cs_=cs_, xi=xi: e.tensor_tensor(out=xr[xi], in0=xr[xi], in1=yo[:, i, cs_], op=ALU.add), reads=[r_yo, r_xr[xi]], writes=[r_xr[xi]])
                S.dma("pool", lambda e, xi=xi, r0=r0, cs_=cs_: e.dma_start(out=out[r0:r0 + 128, cs_], in_=xr[xi]), reads=[r_xr[xi]])


def build_out_program():
    nc = bass.Bass("TRN2", target_bir_lowering=False)
    dr = lambda name, shape, dt, kind="ExternalInput": nc.dram_tensor(name, shape, dt, kind=kind).ap()
    ysel = dr("ysel", [DM, 1024], BF16)
    xrows = dr("xrows", [1024, DM], F32)
    wout = dr("wout", [8, 128, NKC * 512], F32)
    gpost = dr("gpost", [1, DM], F32)
    out = dr("out", [1024, DM], F32, kind="ExternalOutput")
    S = Sched(nc)
    with contextlib.ExitStack() as st:
        for e in ("pe", "act", "dve", "pool"):
            S.sem[e] = st.enter_context(nc.semaphore("s_" + e))
        for i in range(40):
            S.dma_sems.append([st.enter_context(nc.semaphore("d%d" % i)), 0])
        arena_t = st.enter_context(nc.sbuf_tensor("arena", [128, 43008], F32))
        ps = [st.enter_context(nc.psum_tensor("ps%d" % i, [128, 512], F32)) for i in range(8)]
        rps = [Res("ps%d" % i) for i in range(8)]
        A = Arena(arena_t[:, :])
        yv = ysel.rearrange("(kc p) t -> p kc t", p=128)
        out_phase(nc, S, A, ps, rps, lambda qt: yv[:, :, 256 * qt:256 * qt + 256], Res(), wout, xrows, gpost, out)
        S.barrier()
        _replay(nc, S, st)
    return nc


_PROGS = {}


def kernel(**inputs):
    maps8 = prep_inputs(inputs)
    x = np.ascontiguousarray(np.asarray(inputs["x"], dtype=np.float32))
    in_maps = []
    for b in range(2):
        ms = maps8[4 * b:4 * b + 4]
        m = dict(xT=ms[0]["xT"], memT=ms[0]["memT"], cstf=ms[0]["cstf"], cstb=ms[0]["cstb"], wout=ms[0]["wout"], gpost=ms[0]["gpost"],
                 xfull=x[b])
        for k in ("wc", "wkv", "prm", "rowp", "wlora"):
            m[k] = np.ascontiguousarray(np.stack([mm[k] for mm in ms], axis=0))
        in_maps.append(m)
    if "f" not in _PROGS:
        _PROGS["f"] = build_program(stage=3, G=4)
    res = run_bass_kernel_spmd(_PROGS["f"], in_maps, core_ids=[0, 1])
    return np.stack([np.asarray(res.results[b]["out"]) for b in range(2)], axis=0).astype(np.float32)
```

```python
import os
import contextlib
import numpy as np
import concourse.bass as bass
import concourse.mybir as mybir
from concourse.bass_utils import run_bass_kernel_spmd

F32 = mybir.dt.float32
BF16 = mybir.dt.bfloat16
AF = mybir.ActivationFunctionType
ALU = mybir.AluOpType
AX = mybir.AxisListType

ENGS = ("pe", "act", "dve", "pool", "sp")
T = 4096
DM = 4096
NKC = 32
UW = [262, 512, 512, 512, 512, 512, 512, 512]
UOFF = [0]
for _w in UW:
    UOFF.append(UOFF[-1] + _w)
NWC = UOFF[-1]
EXPM05 = 0.6065306597126334


MARKS = []


class Res:
    __slots__ = ("w", "r", "name")

    def __init__(self, name=""):
        self.w = None
        self.r = {}
        self.name = name


class Sched:
    def __init__(self, nc):
        self.nc = nc
        self.ops = {e: [] for e in ENGS}
        self.cnt = {e: 0 for e in ENGS}
        self.seen = {e: {f: 0 for f in ENGS} for e in ENGS}
        self.sem = {}
        self.dma_sems = []
        self._dma_rr = 0

    def _deps(self, e, reads, writes):
        need = {}

        def add(f, c, war=False):
            if f == e and (e == "pe" or e == "sp" or war):
                return
            if need.get(f, 0) < c:
                need[f] = c
        for r in reads:
            if r.w is not None:
                add(*r.w)
        for w in writes:
            if w.w is not None:
                add(*w.w)
            for f, c in w.r.items():
                add(f, c, True)
        return need

    def _emit_waits(self, e, need):
        for f, c in need.items():
            if self.seen[e].get(f, 0) >= c:
                continue
            self.seen[e][f] = c
            sem = self.dma_sems[f[1]][0] if isinstance(f, tuple) else self.sem[f]
            self.ops[e].append(lambda en, sem=sem, c=c: en.wait_ge(sem, c))

    def op(self, e, fn, reads=(), writes=()):
        pr = [r for r in reads if r.name.startswith("ps")]
        if pr:
            reads = [r for r in reads if not r.name.startswith("ps")]
            writes = list(writes) + [r for r in pr if r not in writes]
        self._emit_waits(e, self._deps(e, reads, writes))
        self.cnt[e] += 1
        c = self.cnt[e]
        sem = self.sem[e]
        self.ops[e].append(lambda en, fn=fn, sem=sem: fn(en).then_inc(sem, 1))
        for r in reads:
            r.r[e] = c
        for w in writes:
            w.w = (e, c)
            w.r = {}

    def dma(self, e, fn, reads=(), writes=()):
        self._emit_waits(e, self._deps(e, reads, writes))
        n = len(self.dma_sems)
        idx = self._dma_rr
        self._dma_rr = (idx + 1) % n
        ent = self.dma_sems[idx]
        key = ("dma", idx)
        if ent[1] > 0 and self.seen[e].get(key, 0) < ent[1]:
            self._emit_waits(e, {key: ent[1]})
        ent[1] += 16
        val = ent[1]
        sem = ent[0]
        self.ops[e].append(lambda en, fn=fn, sem=sem: fn(en).then_inc(sem, 16))
        for r in reads:
            r.r[key] = val
        for w in writes:
            w.w = (key, val)
            w.r = {}

    def barrier(self, tag=None):
        MARKS.append((tag, dict(self.cnt)))
        for e in ENGS:
            need = {f: self.cnt[f] for f in ("pe", "act", "dve", "pool") if f != e and self.cnt[f] > 0}
            for idx, ent in enumerate(self.dma_sems):
                if ent[1] > 0:
                    need[("dma", idx)] = ent[1]
            self._emit_waits(e, need)


class Arena:
    def __init__(self, ap):
        self.ap = ap
        self.off = 0

    def f32(self, n):
        v = self.ap[:, self.off:self.off + n]
        self.off += n
        return v

    def bf16(self, n):
        m = (n + 1) // 2
        v = self.ap[:, self.off:self.off + m].bitcast(BF16)
        self.off += m
        return v


def build_program(stage=99, G=1):
    DBG = bool(os.environ.get("KDBG"))
    FUSED = G > 1
    nc = bass.Bass("TRN2", target_bir_lowering=False)
    dr = lambda name, shape, dt, kind="ExternalInput": nc.dram_tensor(name, shape, dt, kind=kind).ap()
    xT = dr("xT", [DM, T], F32)
    memT = dr("memT", [DM, 256], F32)
    wc4 = dr("wc", [G, 128, NKC * NWC], F32)
    wkv4 = dr("wkv", [G, 128, NKC * 512], F32)
    cstf = dr("cstf", [128, 896], F32)
    cstb = dr("cstb", [128, 1024], F32)
    prm4 = dr("prm", [G, 128, 114], F32)
    rowp4 = dr("rowp", [G, 1, 774], F32)
    wlora4 = dr("wlora", [G, 128, 768], F32)
    if FUSED:
        xfull = dr("xfull", [T, DM], F32)
        wout = dr("wout", [8, 128, NKC * 512], F32)
        gpost = dr("gpost", [1, DM], F32)
        out = dr("out", [T, DM], F32, kind="ExternalOutput")
    hT_d = dr("hT_d", [DM, T], BF16, kind="ExternalOutput" if DBG else "Internal")
    hmT_d = dr("hmT_d", [DM, 256], BF16, kind="Internal")
    yT_d = dr("yT_d", [1024 * G, T], BF16, kind="Internal" if FUSED else "ExternalOutput")
    gsel = {"g": 0}
    yall_d = dr("yall_d", [4096, T], BF16, kind="Internal")
    lora_d = dr("lora_d", [2, 128, T], BF16, kind="Internal")
    dbg = dr("dbg", [128, 4096], F32, kind="ExternalOutput") if DBG else None

    S = Sched(nc)
    with contextlib.ExitStack() as st:
        for e in ("pe", "act", "dve", "pool"):
            S.sem[e] = st.enter_context(nc.semaphore("s_" + e))
        for i in range(40):
            S.dma_sems.append([st.enter_context(nc.semaphore("d%d" % i)), 0])
        ARENA_N = 43008
        arena_t = st.enter_context(nc.sbuf_tensor("arena", [128, ARENA_N], F32))
        ps = [st.enter_context(nc.psum_tensor("ps%d" % i, [128, 512], F32)) for i in range(8)]
        rps = [Res("ps%d" % i) for i in range(8)]
        A = Arena(arena_t[:, :])

        c_f = A.f32(896); r_cf = Res()
        c_b = A.bf16(1024); r_cb = Res()
        p_f = A.f32(114); r_pf = Res()
        rowb = A.f32(774); r_rowb = Res()
        wl_b = A.bf16(768); r_wlb = Res()
        cum = A.f32(192).rearrange("p (b h) -> p b h", h=6); r_cum = Res()
        refbc = A.f32(192).rearrange("p (b h) -> p b h", h=6); r_ref = Res()
        PERS = A.off

        ident_f = c_f[:, 0:128]
        tri_f = c_f[:, 128:256]
        ones_f = c_f[:, 256:384]
        rmask = c_f[:, 384:896]
        ident_b = c_b[:, 0:128]
        ones_b = c_b[:, 128:256]
        mask_fox = c_b[:, 256:384]
        maskP = c_b[:, 384:896]
        maskN0 = c_b[:, 896:1024]
        gpre = p_f[:, 0:32]
        gmem = p_f[:, 32:64]
        lnw_bc = rowb[:, 0:384]
        lnb_bc = rowb[:, 384:768]
        bf_bc = rowb[:, 768:774]

        S.dma("sp", lambda e: e.dma_start(out=c_f, in_=cstf), writes=[r_cf])
        S.dma("pool", lambda e: e.dma_start(out=c_b, in_=cstb), writes=[r_cb])
        def load_params(g):
            gsel["g"] = g
            S.dma("sp", lambda e: e.dma_start(out=p_f, in_=prm4[g]), writes=[r_pf])
            S.dma("sp", lambda e: e.dma_start(out=rowb, in_=rowp4[g, 0:1, :].to_broadcast([128, 774])), writes=[r_rowb])
            S.dma("pool", lambda e: e.dma_start(out=wl_b, in_=wlora4[g]), writes=[r_wlb])
        load_params(0)

        hTv = hT_d.rearrange("(kc p) t -> p kc t", p=128)
        r_hTd = Res()
        r_hmTd = Res()
        r_yTd = Res()

        A.off = PERS
        xb = [A.f32(NKC * 256).rearrange("p (kc t) -> p kc t", t=256) for _ in range(2)]
        r_xb = [[Res(), Res()] for _ in range(2)]
        sq = A.bf16(NKC * 256).rearrange("p (kc t) -> p kc t", t=256); r_sq = Res()
        hb = [A.bf16(NKC * 256).rearrange("p (kc t) -> p kc t", t=256) for _ in range(2)]
        r_hb = [Res(), Res()]
        rstd = A.f32(256); r_rstd = Res()
        xTv = xT.rearrange("(kc p) t -> p kc t", p=128)
        mTv = memT.rearrange("(kc p) t -> p kc t", p=128)
        hmTv = hmT_d.rearrange("(kc p) t -> p kc t", p=128)

        def p1_load(i):
            b = i % 2
            src = mTv if i == 16 else xTv[:, :, i * 256:(i + 1) * 256]
            for hf in range(2):
                S.dma("sp", lambda e, b=b, hf=hf, src=src: e.dma_start(out=xb[b][:, 16 * hf:16 * hf + 16, :], in_=src[:, 16 * hf:16 * hf + 16, :]),
                      writes=[r_xb[b][hf]])

        p1_load(0)
        for i in range(17):
            b = i % 2
            if i + 1 < 17:
                p1_load(i + 1)
            gvec = gmem if i == 16 else gpre
            for hf in range(2):
                S.op("act", lambda e, b=b, hf=hf: e.activation(out=sq[:, 16 * hf:16 * hf + 16, :], in_=xb[b][:, 16 * hf:16 * hf + 16, :], func=AF.Square),
                     reads=[r_xb[b][hf]], writes=[r_sq])
            for kc in range(NKC):
                S.op("pe", lambda e, kc=kc: e.matmul(ps[0][:, 0:256], lhsT=ones_b, rhs=sq[:, kc, :], start=(kc == 0), stop=(kc == NKC - 1)),
                     reads=[r_sq, r_cb], writes=[rps[0]])
            S.op("dve", lambda e: e.tensor_scalar(out=rstd, in0=ps[0][:, 0:256], scalar1=1.0 / DM, scalar2=1e-6, op0=ALU.mult, op1=ALU.add),
                 reads=[rps[0]], writes=[r_rstd])
            S.op("act", lambda e: e.activation(out=rstd, in_=rstd, func=AF.Sqrt), reads=[r_rstd], writes=[r_rstd])
            S.op("dve", lambda e: e.reciprocal(out=rstd, in_=rstd), reads=[r_rstd], writes=[r_rstd])
            for kc in range(NKC):
                S.op("dve", lambda e, b=b, kc=kc, gvec=gvec: e.scalar_tensor_tensor(out=hb[b][:, kc, :], in0=xb[b][:, kc, :], scalar=gvec[:, kc:kc + 1], in1=rstd,
                                                                                 op0=ALU.mult, op1=ALU.mult),
                     reads=[r_xb[b][kc // 16], r_rstd, r_pf], writes=[r_hb[b]])
            if i == 16:
                S.dma("pool", lambda e, b=b: e.dma_start(out=hmTv, in_=hb[b]), reads=[r_hb[b]], writes=[r_hmTd])
            else:
                S.dma("pool", lambda e, b=b, i=i: e.dma_start(out=hTv[:, :, i * 256:(i + 1) * 256], in_=hb[b]), reads=[r_hb[b]], writes=[r_hTd])
        S.barrier()

        A.off = PERS
        Wb = A.bf16(NKC * 512); r_W = Res()
        hbuf = [A.bf16(NKC * 512).rearrange("p (kc t) -> p kc t", t=512) for _ in range(2)]
        r_hbuf = [[Res(), Res()] for _ in range(2)]
        WORK = A.off

        def load_unit(u, src=None, n=None):
            if src is None:
                n = UW[u]
                src = wc4[gsel["g"], :, NKC * UOFF[u]:NKC * (UOFF[u] + n)]
            tot = NKC * n
            o = 0
            while o < tot:
                m = min(2048, tot - o)
                S.dma("pool", lambda e, o=o, m=m, src=src: e.dma_start(out=Wb[:, o:o + m], in_=src[:, o:o + m]), writes=[r_W])
                o += m
            return Wb[:, 0:tot].rearrange("p (kc n) -> p kc n", n=n)

        hcount = [0]

        def h_load(tb):
            b = hcount[0] % 2
            hcount[0] += 1
            for hf in range(2):
                S.dma("sp", lambda e, b=b, hf=hf, tb=tb: e.dma_start(out=hbuf[b][:, 16 * hf:16 * hf + 16, :], in_=hTv[:, 16 * hf:16 * hf + 16, tb * 512:(tb + 1) * 512]),
                      reads=[r_hTd], writes=[r_hbuf[b][hf]])
            return b

        def proj_fm(Wv, c0, m, hb_i, pst, prs, n0=0, n=512):
            for kc in range(NKC):
                S.op("pe", lambda e, kc=kc: e.matmul(pst[0:m, 0:n], lhsT=Wv[:, kc, c0:c0 + m], rhs=hbuf[hb_i][:, kc, n0:n0 + n], start=(kc == 0), stop=(kc == NKC - 1)),
                     reads=[r_W, r_hbuf[hb_i][kc // 16]], writes=[prs])

        def proj_tm(Wv, c0, m, hb_i, tt, pso, prs):
            for kc in range(NKC):
                S.op("pe", lambda e, kc=kc: e.matmul(pso, lhsT=hbuf[hb_i][:, kc, 128 * tt:128 * tt + 128], rhs=Wv[:, kc, c0:c0 + m], start=(kc == 0), stop=(kc == NKC - 1)),
                     reads=[r_W, r_hbuf[hb_i][kc // 16]], writes=[prs])

        def shift_block(praw, r_praw, mu_ap, npart, outf, r_out, eng_copy="act"):
            S.op("dve", lambda e: e.tensor_tensor(out=outf, in0=praw[0:npart, 0:512], in1=praw[0:npart, 1:513], op=ALU.subtract),
                 reads=[r_praw], writes=[r_out])
            S.op("dve", lambda e: e.scalar_tensor_tensor(out=outf, in0=outf, scalar=mu_ap, in1=praw[0:npart, 1:513], op0=ALU.mult, op1=ALU.add),
                 reads=[r_praw, r_out, r_pf], writes=[r_out])
            S.op(eng_copy, (lambda e: e.activation(out=praw[0:npart, 0:1], in_=praw[0:npart, 512:513], func=AF.Copy)) if eng_copy == "act"
                 else (lambda e: e.tensor_copy(out=praw[0:npart, 0:1], in_=praw[0:npart, 512:513])), reads=[r_praw], writes=[r_praw])

        r_lorad = Res()

        def unit0():
            A.off = WORK
            pr_wl = A.f32(513); r_prwl = Res()
            pr_al = A.f32(513); r_pral = Res()
            tmpf = A.f32(512); r_tmpf = Res()
            lob = [[A.bf16(512) for _ in range(2)] for _ in range(2)]; r_lob = [[Res(), Res()], [Res(), Res()]]
            lfraw = A.f32(192).rearrange("p (b h) -> p b h", h=6); r_lf = Res()
            tot = A.f32(192).rearrange("p (b h) -> p b h", h=6); r_tot = Res()
            blkoff = A.f32(192).rearrange("p (b h) -> p b h", h=6); r_boff = Res()
            S.op("dve", lambda e: e.memset(pr_wl[:, 0:1], 0.0), writes=[r_prwl])
            S.op("dve", lambda e: e.memset(pr_al[:, 0:1], 0.0), writes=[r_pral])
            Wv = load_unit(0)
            nb = h_load(0)
            for tb in range(8):
                hb_i = nb
                if tb + 1 < 8:
                    nb = h_load(tb + 1)
                proj_fm(Wv, 0, 128, hb_i, ps[0], rps[0])
                S.op("act", lambda e: e.activation(out=pr_wl[:, 1:513], in_=ps[0][:, :], func=AF.Copy), reads=[rps[0]], writes=[r_prwl])
                proj_fm(Wv, 128, 128, hb_i, ps[1], rps[1])
                S.op("dve", lambda e: e.tensor_copy(out=pr_al[:, 1:513], in_=ps[1][:, :]), reads=[rps[1]], writes=[r_pral])
                for tt in range(4):
                    proj_tm(Wv, 256, 6, hb_i, tt, ps[2][:, 6 * tt:6 * tt + 6], rps[2])
                S.op("act", lambda e, tb=tb: e.activation(out=lfraw[:, 4 * tb:4 * tb + 4, :], in_=ps[2][:, 0:24].rearrange("p (b h) -> p b h", h=6), func=AF.Copy),
                     reads=[rps[2]], writes=[r_lf])
                shift_block(pr_wl, r_prwl, p_f[:, 64:65], 128, tmpf, r_tmpf)
                S.op("act", lambda e, tb=tb: e.activation(out=lob[0][tb % 2], in_=tmpf, func=AF.Tanh), reads=[r_tmpf], writes=[r_lob[0][tb % 2]])
                S.dma("pool", lambda e, tb=tb: e.dma_start(out=lora_d[0, :, tb * 512:(tb + 1) * 512], in_=lob[0][tb % 2]), reads=[r_lob[0][tb % 2]], writes=[r_lorad])
                shift_block(pr_al, r_pral, p_f[:, 65:66], 128, tmpf, r_tmpf)
                S.op("act", lambda e, tb=tb: e.activation(out=lob[1][tb % 2], in_=tmpf, func=AF.Copy), reads=[r_tmpf], writes=[r_lob[1][tb % 2]])
                S.dma("pool", lambda e, tb=tb: e.dma_start(out=lora_d[1, :, tb * 512:(tb + 1) * 512], in_=lob[1][tb % 2]), reads=[r_lob[1][tb % 2]], writes=[r_lorad])
            S.op("dve", lambda e: e.tensor_tensor(out=lfraw, in0=lfraw, in1=bf_bc.unsqueeze(1).to_broadcast([128, 32, 6]), op=ALU.add), reads=[r_lf, r_rowb], writes=[r_lf])
            S.op("act", lambda e: e.activation(out=lfraw, in_=lfraw, func=AF.Sigmoid), reads=[r_lf], writes=[r_lf])
            S.op("act", lambda e: e.activation(out=lfraw, in_=lfraw, func=AF.Ln), reads=[r_lf], writes=[r_lf])
            lf2 = lfraw.rearrange("p b h -> p (b h)")
            S.op("pe", lambda e: e.matmul(ps[0][:, 0:192], lhsT=tri_f, rhs=lf2, start=True, stop=True), reads=[r_lf, r_cf], writes=[rps[0]])
            S.op("pe", lambda e: e.matmul(ps[1][:, 0:192], lhsT=ones_f, rhs=lf2, start=True, stop=True), reads=[r_lf, r_cf], writes=[rps[1]])
            S.op("act", lambda e: e.activation(out=tot.rearrange("p b h -> p (b h)"), in_=ps[1][:, 0:192], func=AF.Copy), reads=[rps[1]], writes=[r_tot])
            S.op("dve", lambda e: e.memset(blkoff[:, 0, :], 0.0), writes=[r_boff])
            for b in range(31):
                S.op("dve", lambda e, b=b: e.tensor_tensor(out=blkoff[:, b + 1, :], in0=blkoff[:, b, :], in1=tot[:, b, :], op=ALU.add), reads=[r_tot, r_boff], writes=[r_boff])
            S.op("dve", lambda e: e.tensor_tensor(out=cum.rearrange("p b h -> p (b h)"), in0=ps[0][:, 0:192], in1=blkoff.rearrange("p b h -> p (b h)"), op=ALU.add),
                 reads=[rps[0], r_boff], writes=[r_cum])
            S.op("dve", lambda e: e.scalar_tensor_tensor(out=refbc.rearrange("p b h -> p (b h)"), in0=tot.rearrange("p b h -> p (b h)"), scalar=0.5,
                                                         in1=blkoff.rearrange("p b h -> p (b h)"), op0=ALU.mult, op1=ALU.add), reads=[r_tot, r_boff], writes=[r_ref])
            if DBG:
                S.dma("sp", lambda e: e.dma_start(out=dbg[:, 0:192], in_=cum.rearrange("p b h -> p (b h)")), reads=[r_cum])
                S.dma("sp", lambda e: e.dma_start(out=dbg[:, 192:384], in_=refbc.rearrange("p b h -> p (b h)")), reads=[r_ref])
                S.dma("sp", lambda e: e.dma_start(out=dbg[:, 384:576], in_=lfraw.rearrange("p b h -> p (b h)")), reads=[r_lf])
            S.barrier()


        def fox_unit(j):
            A.off = WORK
            yr0 = 1024 * gsel["g"]
            qT = A.bf16(T); r_q = Res()
            kT = A.bf16(T); r_k = Res()
            sgT = A.bf16(T); r_sg = Res()
            vaug = A.bf16(32 * 256).rearrange("p (b h d) -> p b h d", h=2, d=128); r_v = Res()
            biasT = A.f32(2 * 32 * 32).rearrange("p (h k q) -> p h k q", h=2, k=32); r_bias = Res()
            NROT = 6
            Eb = [A.bf16(512) for _ in range(NROT)]; r_E = [Res() for _ in range(NROT)]
            assert A.off + 512 + 512 + 512 <= ARENA_N, A.off
            dsh = A.f32(512); r_dsh = Res()
            ytmp = A.f32(512); r_ytmp = Res()
            ybuf = [A.bf16(512) for _ in range(2)]; r_yb = [Res(), Res()]
            S.op("dve", lambda e: e.memset(vaug[:, :, 0, 64:128], 1.0), writes=[r_v])
            S.op("dve", lambda e: e.memset(vaug[:, :, 1, 0:64], 1.0), writes=[r_v])
            for h in range(2):
                hh = 2 * j + h
                for kb in range(32):
                    S.op("dve", lambda e, h=h, hh=hh, kb=kb: e.tensor_scalar(out=biasT[:, h, kb, kb:32], in0=refbc[:, kb:32, hh], scalar1=cum[:, kb, hh:hh + 1], scalar2=None,
                                                                             op0=ALU.subtract), reads=[r_ref, r_cum], writes=[r_bias])
            Wv = load_unit(1 + j)
            nb = h_load(0)
            for tb in range(8):
                hb_i = nb
                if tb + 1 < 8:
                    nb = h_load(tb + 1)
                sl = slice(tb * 512, (tb + 1) * 512)
                proj_fm(Wv, 0, 128, hb_i, ps[5], rps[5])
                S.op("act", lambda e, sl=sl: e.activation(out=qT[:, sl], in_=ps[5][:, :], func=AF.Copy, scale=0.125), reads=[rps[5]], writes=[r_q])
                proj_fm(Wv, 128, 128, hb_i, ps[6], rps[6])
                S.op("dve", lambda e, sl=sl: e.tensor_copy(out=kT[:, sl], in_=ps[6][:, :]), reads=[rps[6]], writes=[r_k])
                proj_fm(Wv, 256, 128, hb_i, ps[5], rps[5])
                S.op("act", lambda e, sl=sl: e.activation(out=sgT[:, sl], in_=ps[5][:, :], func=AF.Silu), reads=[rps[5]], writes=[r_sg])
                for tt in range(4):
                    proj_tm(Wv, 384, 128, hb_i, tt, ps[7][:, 128 * tt:128 * tt + 128], rps[7])
                pv = ps[7][:, :].rearrange("p (b d) -> p b d", d=128)
                S.op("act", lambda e, tb=tb, pv=pv: e.activation(out=vaug[:, 4 * tb:4 * tb + 4, 0, 0:64], in_=pv[:, :, 0:64], func=AF.Copy), reads=[rps[7]], writes=[r_v])
                S.op("dve", lambda e, tb=tb, pv=pv: e.tensor_copy(out=vaug[:, 4 * tb:4 * tb + 4, 1, 64:128], in_=pv[:, :, 64:128]), reads=[rps[7]], writes=[r_v])
            steps = []
            for Q4 in range(8):
                yb_i = Q4 % 2
                for h in range(2):
                    nkb = 4 * Q4 + 4
                    for kb in range(nkb):
                        steps.append((Q4, h, kb, nkb, yb_i))

            def emit_qk(i):
                Q4, h, kb, nkb, yb_i = steps[i]
                pr = slice(64 * h, 64 * h + 64)
                qlo = max(512 * Q4, 128 * kb)
                n = 512 * (Q4 + 1) - qlo
                sb = 2 + i % NROT
                S.op("pe", lambda e, sb=sb, pr=pr, kb=kb, qlo=qlo, n=n: e.matmul(ps[sb][:, 0:n], lhsT=kT[pr, 128 * kb:128 * kb + 128], rhs=qT[pr, qlo:qlo + n], start=True, stop=True),
                     reads=[r_q, r_k], writes=[rps[sb]])

            def emit_rest(i):
                Q4, h, kb, nkb, yb_i = steps[i]
                pr = slice(64 * h, 64 * h + 64)
                po, rpo = ps[h], rps[h]
                qlo = max(512 * Q4, 128 * kb)
                n = 512 * (Q4 + 1) - qlo
                c0 = qlo - 512 * Q4
                sb = 2 + i % NROT
                eb = i % NROT
                for qi in range(n // 128):
                    qb = qlo // 128 + qi
                    S.op("act", lambda e, sb=sb, eb=eb, qi=qi, h=h, kb=kb, qb=qb: e.activation(out=Eb[eb][:, 128 * qi:128 * qi + 128], in_=ps[sb][:, 128 * qi:128 * qi + 128], func=AF.Exp,
                                                                                             bias=biasT[:, h, kb, qb:qb + 1], scale=1.0),
                         reads=[rps[sb], r_bias], writes=[r_E[eb]])
                if kb >= 4 * Q4:
                    S.op("dve", lambda e, eb=eb: e.tensor_tensor(out=Eb[eb][:, 0:128], in0=Eb[eb][:, 0:128], in1=mask_fox, op=ALU.mult), reads=[r_E[eb], r_cb], writes=[r_E[eb]])
                S.op("pe", lambda e, po=po, c0=c0, n=n, kb=kb, h=h, eb=eb, nkb=nkb: e.matmul(po[:, c0:c0 + n], lhsT=vaug[:, kb, h, :], rhs=Eb[eb][:, 0:n], start=(kb == 0), stop=(kb == nkb - 1)),
                     reads=[r_v, r_E[eb]], writes=[rpo])
                if kb == nkb - 1:
                    dn = slice(64, 128) if h == 0 else slice(0, 64)
                    S.op("act", lambda e, po=po, pr=pr, dn=dn: e.activation(out=dsh[pr, :], in_=po[dn, :], func=AF.Copy), reads=[rpo], writes=[r_dsh])
                    S.op("dve", lambda e, pr=pr: e.reciprocal(out=dsh[pr, :], in_=dsh[pr, :]), reads=[r_dsh], writes=[r_dsh])
                    S.op("dve", lambda e, po=po, pr=pr: e.tensor_tensor(out=ytmp[pr, :], in0=po[pr, :], in1=dsh[pr, :], op=ALU.mult), reads=[rpo, r_dsh], writes=[r_ytmp])
                    S.op("dve", lambda e, pr=pr, yb_i=yb_i, Q4=Q4: e.tensor_tensor(out=ybuf[yb_i][pr, :], in0=ytmp[pr, :], in1=sgT[pr, 512 * Q4:512 * Q4 + 512], op=ALU.mult),
                         reads=[r_ytmp, r_sg], writes=[r_yb[yb_i]])
                    if h == 1:
                        S.dma("pool", lambda e, yb_i=yb_i, Q4=Q4: e.dma_start(out=yT_d[yr0 + 384 + 128 * j:yr0 + 512 + 128 * j, 512 * Q4:512 * Q4 + 512], in_=ybuf[yb_i]), reads=[r_yb[yb_i]], writes=[r_yTd])

            LOOK = 2
            for i in range(min(LOOK, len(steps))):
                emit_qk(i)
            for i in range(len(steps)):
                if i + LOOK < len(steps):
                    emit_qk(i + LOOK)
                emit_rest(i)
            S.barrier()

        def mem_unit():
            A.off = WORK
            yr0 = 1024 * gsel["g"]
            hm = A.bf16(NKC * 256).rearrange("p (kc t) -> p kc t", t=256); r_hm = Res()
            mkT = A.bf16(512).rearrange("p (d m) -> p d m", m=256); r_mk = Res()
            mv = A.bf16(512).rearrange("p (m d) -> p m d", d=256); r_mv = Res()
            mqT = A.bf16(1024).rearrange("p (d t) -> p d t", t=512); r_mq = Res()
            sgm = A.bf16(1024).rearrange("p (d t) -> p d t", t=512); r_sgm = Res()
            Em = A.bf16(1024).rearrange("p (m t) -> p m t", t=512); r_Em = Res()
            rec = A.f32(512); r_rec = Res()
            ytmp = A.f32(512); r_ytmp = Res()
            ybuf = [A.bf16(512) for _ in range(2)]; r_yb = [Res(), Res()]
            S.dma("sp", lambda e: e.dma_start(out=hm, in_=hmTv), reads=[r_hmTd], writes=[r_hm])
            Wk = load_unit(None, src=wkv4[gsel["g"]], n=512)
            for dt in range(2):
                for kc in range(NKC):
                    S.op("pe", lambda e, dt=dt, kc=kc: e.matmul(ps[0][:, 0:256], lhsT=Wk[:, kc, 128 * dt:128 * dt + 128], rhs=hm[:, kc, :], start=(kc == 0), stop=(kc == NKC - 1)),
                         reads=[r_W, r_hm], writes=[rps[0]])
                S.op("act", lambda e, dt=dt: e.activation(out=mkT[:, dt, :], in_=ps[0][:, 0:256], func=AF.Copy), reads=[rps[0]], writes=[r_mk])
            for mt in range(2):
                for kc in range(NKC):
                    S.op("pe", lambda e, mt=mt, kc=kc: e.matmul(ps[1][:, 0:256], lhsT=hm[:, kc, 128 * mt:128 * mt + 128], rhs=Wk[:, kc, 256:512], start=(kc == 0), stop=(kc == NKC - 1)),
                         reads=[r_W, r_hm], writes=[rps[1]])
                S.op("dve", lambda e, mt=mt: e.tensor_copy(out=mv[:, mt, :], in_=ps[1][:, 0:256]), reads=[rps[1]], writes=[r_mv])
            Wv = load_unit(4)
            nb = h_load(0)
            for tb in range(8):
                hb_i = nb
                if tb + 1 < 8:
                    nb = h_load(tb + 1)
                for dt in range(2):
                    proj_fm(Wv, 128 * dt, 128, hb_i, ps[2 + dt], rps[2 + dt])
                    S.op("act", lambda e, dt=dt: e.activation(out=mqT[:, dt, :], in_=ps[2 + dt][:, :], func=AF.Copy, scale=1.0 / 16.0), reads=[rps[2 + dt]], writes=[r_mq])
                for dt in range(2):
                    proj_fm(Wv, 256 + 128 * dt, 128, hb_i, ps[2 + dt], rps[2 + dt])
                    S.op("act", lambda e, dt=dt: e.activation(out=sgm[:, dt, :], in_=ps[2 + dt][:, :], func=AF.Silu), reads=[rps[2 + dt]], writes=[r_sgm])
                for mt in range(2):
                    for dt in range(2):
                        S.op("pe", lambda e, mt=mt, dt=dt: e.matmul(ps[4][:, :], lhsT=mkT[:, dt, 128 * mt:128 * mt + 128], rhs=mqT[:, dt, :], start=(dt == 0), stop=(dt == 1)),
                             reads=[r_mk, r_mq], writes=[rps[4]])
                    S.op("act", lambda e, mt=mt: e.activation(out=Em[:, mt, :], in_=ps[4][:, :], func=AF.Exp), reads=[rps[4]], writes=[r_Em])
                for mt in range(2):
                    S.op("pe", lambda e, mt=mt: e.matmul(ps[5][:, :], lhsT=ones_b, rhs=Em[:, mt, :], start=(mt == 0), stop=(mt == 1)), reads=[r_Em, r_cb], writes=[rps[5]])
                S.op("dve", lambda e: e.reciprocal(out=rec, in_=ps[5][:, :]), reads=[rps[5]], writes=[r_rec])
                for dt in range(2):
                    for mt in range(2):
                        S.op("pe", lambda e, mt=mt, dt=dt: e.matmul(ps[6 + dt][:, :], lhsT=mv[:, mt, 128 * dt:128 * dt + 128], rhs=Em[:, mt, :], start=(mt == 0), stop=(mt == 1)),
                             reads=[r_mv, r_Em], writes=[rps[6 + dt]])
                    S.op("dve", lambda e, dt=dt: e.tensor_tensor(out=ytmp, in0=ps[6 + dt][:, :], in1=rec, op=ALU.mult), reads=[rps[6 + dt], r_rec], writes=[r_ytmp])
                    S.op("dve", lambda e, dt=dt: e.tensor_tensor(out=ybuf[dt], in0=ytmp, in1=sgm[:, dt, :], op=ALU.mult), reads=[r_ytmp, r_sgm], writes=[r_yb[dt]])
                    S.dma("pool", lambda e, dt=dt, tb=tb: e.dma_start(out=yT_d[yr0 + 768 + 128 * dt:yr0 + 896 + 128 * dt, 512 * tb:512 * tb + 512], in_=ybuf[dt]), reads=[r_yb[dt]], writes=[r_yTd])
            S.barrier()

        def rwkv_unit(j):
            A.off = WORK - NKC * 256
            H = 64
            lo = [[A.bf16(512) for _ in range(2)] for _ in range(2)]; r_lo = [[Res(), Res()], [Res(), Res()]]
            f = lambda n: A.f32(n)
            praw = [[f(513) for _ in range(3)] for _ in range(2)]
            r_praw = [[Res() for _ in range(3)] for _ in range(2)]
            ST = [f(64) for _ in range(2)]; r_ST = [Res(), Res()]
            rs, ks, vs = f(512), f(512), f(512); r_rs, r_ks, r_vs = Res(), Res(), Res()
            alpha, lw, cumc, cume = f(512), f(512), f(512), f(512)
            r_alpha, r_lw, r_cumc, r_cume = Res(), Res(), Res(), Res()
            We, Winv = f(512), f(512); r_We, r_Winv = Res(), Res()
            kk, t1, km = f(512), f(512), f(512); r_kk, r_t1, r_km = Res(), Res(), Res()
            sgT = [A.bf16(512) for _ in range(2)]; r_sg = [Res(), Res()]
            Wi = [f(512) for _ in range(2)]; r_Wi = [Res(), Res()]
            rkT = [f(512) for _ in range(2)]; r_rk = [Res(), Res()]
            aT = [A.bf16(512) for _ in range(2)]; r_aT = [Res(), Res()]
            rT = [A.bf16(512) for _ in range(2)]; r_rT = [Res(), Res()]
            bT = [A.bf16(512) for _ in range(2)]; r_bT = [Res(), Res()]
            kTt = [A.bf16(512) for _ in range(2)]; r_kTt = [Res(), Res()]
            vT = [A.bf16(512) for _ in range(2)]; r_vT = [Res(), Res()]
            tm = [A.bf16(8 * 4 * 64).rearrange("p (c a d) -> p c a d", a=4, d=64) for _ in range(2)]; r_tm = [Res(), Res()]
            Pm = A.bf16(1024).rearrange("p (q m t) -> p q m t", q=4, m=4); r_Pm = Res()
            Dk = [A.bf16(512).rearrange("p (q s t) -> p q s t", q=4, s=2) for _ in range(2)]; r_Dk = [Res(), Res()]
            Xf = f(512).rearrange("p (q d) -> p q d", q=4); r_Xf = Res()
            Xb = A.bf16(512).rearrange("p (q d) -> p q d", q=4); r_Xb = Res()
            RhT = f(256).rearrange("p (q t) -> p q t", q=4); r_Rh = Res()
            GpT = f(256).rearrange("p (q t) -> p q t", q=4); r_Gp = Res()
            HmW = f(256).rearrange("p (q t) -> p q t", q=4); r_Hm = Res()
            Y0 = f(256).rearrange("p (q t) -> p q t", q=4); r_Y0 = Res()
            Ytok = [f(512).rearrange("p (c d) -> p c d", d=64) for _ in range(2)]; r_Yt = [Res(), Res()]
            yc = f(512).rearrange("p (c d) -> p c d", d=64); r_yc = Res()
            ysq = f(512).rearrange("p (c d) -> p c d", d=64); r_ysq = Res()
            st1, st2, bon = f(8), f(8), f(8); r_st1, r_st2, r_bon = Res(), Res(), Res()
            ybuf = [A.bf16(512) for _ in range(2)]; r_yb = [Res(), Res()]
            assert A.off <= ARENA_N, A.off
            for h in range(2):
                for a in range(3):
                    S.op("dve", lambda e, h=h, a=a: e.memset(praw[h][a][:, 0:1], 0.0), writes=[r_praw[h][a]])
                S.op("dve", lambda e, h=h: e.memset(ST[h][0:H, :], 0.0), writes=[r_ST[h]])
            Wv = load_unit(5 + j)
            for tb in range(8):
                hcount[0] = 0
                hb_i = h_load(tb)
                t0 = tb * 512
                lb = tb % 2
                for w_ in range(2):
                    S.dma("sp", lambda e, w_=w_, lb=lb, t0=t0: e.dma_start(out=lo[w_][lb], in_=lora_d[w_, :, t0:t0 + 512]), reads=[r_lorad], writes=[r_lo[w_][lb]])
                for a in range(3):
                    pb = a % 2
                    proj_fm(Wv, 128 * a, 128, hb_i, ps[pb], rps[pb])
                    S.op("act", lambda e, a=a, pb=pb: e.activation(out=praw[0][a][0:H, 1:513], in_=ps[pb][0:H, :], func=AF.Copy), reads=[rps[pb]], writes=[r_praw[0][a]])
                    S.op("dve", lambda e, a=a, pb=pb: e.tensor_copy(out=praw[1][a][0:H, 1:513], in_=ps[pb][64:128, :]), reads=[rps[pb]], writes=[r_praw[1][a]])
                proj_fm(Wv, 384, 128, hb_i, ps[1], rps[1])
                S.op("act", lambda e: e.activation(out=sgT[0][0:H, :], in_=ps[1][0:H, :], func=AF.Silu), reads=[rps[1]], writes=[r_sg[0]])
                S.op("act", lambda e: e.activation(out=sgT[1][0:H, :], in_=ps[1][64:128, :], func=AF.Silu), reads=[rps[1]], writes=[r_sg[1]])
                for h in range(2):
                    hh = 2 * j + h
                    pc = 66 + 8 * hh
                    PV = [p_f[0:H, pc + k_:pc + k_ + 1] for k_ in range(8)]
                    cw = 64 * hh
                    shift_block(praw[h][0], r_praw[h][0], PV[0], H, rs[0:H, :], r_rs)
                    shift_block(praw[h][1], r_praw[h][1], PV[1], H, ks[0:H, :], r_ks)
                    shift_block(praw[h][2], r_praw[h][2], PV[2], H, vs[0:H, :], r_vs)
                    S.op("pe", lambda e, cw=cw, lb=lb: e.matmul(ps[0][0:H, :], lhsT=wl_b[:, cw:cw + 64], rhs=lo[0][lb], start=True, stop=True),
                         reads=[r_wlb, r_lo[0][lb]], writes=[rps[0]])
                    S.op("act", lambda e, b_=PV[3]: e.activation(out=lw[0:H, :], in_=ps[0][0:H, :], func=AF.Sigmoid, bias=b_, scale=1.0), reads=[rps[0], r_pf], writes=[r_lw])
                    S.op("pe", lambda e, cw=cw, lb=lb: e.matmul(ps[1][0:H, :], lhsT=wl_b[:, 384 + cw:384 + cw + 64], rhs=lo[1][lb], start=True, stop=True),
                         reads=[r_wlb, r_lo[1][lb]], writes=[rps[1]])
                    S.op("act", lambda e, b_=PV[4]: e.activation(out=alpha[0:H, :], in_=ps[1][0:H, :], func=AF.Sigmoid, bias=b_, scale=1.0), reads=[rps[1], r_pf], writes=[r_alpha])
                    S.op("dve", lambda e: e.tensor_scalar(out=lw[0:H, :], in0=lw[0:H, :], scalar1=-EXPM05, scalar2=None, op0=ALU.mult), reads=[r_lw], writes=[r_lw])
                    S.op("dve", lambda e: e.tensor_tensor_scan(out=cumc[0:H, :], data0=rmask[0:H, :], data1=lw[0:H, :], initial=0.0, op0=ALU.mult, op1=ALU.add),
                         reads=[r_lw, r_cf], writes=[r_cumc])
                    S.op("dve", lambda e: e.tensor_tensor(out=cume[0:H, :], in0=cumc[0:H, :], in1=lw[0:H, :], op=ALU.subtract), reads=[r_cumc, r_lw], writes=[r_cume])
                    S.op("act", lambda e, h=h: e.activation(out=Wi[h][0:H, :], in_=cumc[0:H, :], func=AF.Exp), reads=[r_cumc], writes=[r_Wi[h]])
                    S.op("act", lambda e: e.activation(out=We[0:H, :], in_=cume[0:H, :], func=AF.Exp), reads=[r_cume], writes=[r_We])
                    S.op("act", lambda e: e.activation(out=Winv[0:H, :], in_=cumc[0:H, :], func=AF.Exp, scale=-1.0), reads=[r_cumc], writes=[r_Winv])
                    S.op("dve", lambda e, s_=PV[5]: e.tensor_scalar(out=kk[0:H, :], in0=ks[0:H, :], scalar1=s_, scalar2=None, op0=ALU.mult), reads=[r_ks, r_pf], writes=[r_kk])
                    S.op("act", lambda e: e.activation(out=t1[0:H, :], in_=kk[0:H, :], func=AF.Square), reads=[r_kk], writes=[r_t1])
                    S.op("pe", lambda e: e.matmul(ps[0][0:H, :], lhsT=ones_f[0:H, 0:H], rhs=t1[0:H, :], start=True, stop=True), reads=[r_t1, r_cf], writes=[rps[0]])
                    S.op("act", lambda e: e.activation(out=t1[0:H, :], in_=ps[0][0:H, :], func=AF.Sqrt), reads=[rps[0]], writes=[r_t1])
                    S.op("dve", lambda e: e.tensor_scalar(out=t1[0:H, :], in0=t1[0:H, :], scalar1=1e-12, scalar2=None, op0=ALU.max), reads=[r_t1], writes=[r_t1])
                    S.op("dve", lambda e: e.reciprocal(out=t1[0:H, :], in_=t1[0:H, :]), reads=[r_t1], writes=[r_t1])
                    S.op("dve", lambda e: e.tensor_tensor(out=kk[0:H, :], in0=kk[0:H, :], in1=t1[0:H, :], op=ALU.mult), reads=[r_kk, r_t1], writes=[r_kk])
                    S.op("dve", lambda e, h=h: e.scalar_tensor_tensor(out=aT[h][0:H, :], in0=kk[0:H, :], scalar=-1.0, in1=We[0:H, :], op0=ALU.mult, op1=ALU.mult), reads=[r_kk, r_We], writes=[r_aT[h]])
                    S.op("dve", lambda e: e.tensor_tensor(out=t1[0:H, :], in0=kk[0:H, :], in1=alpha[0:H, :], op=ALU.mult), reads=[r_kk, r_alpha], writes=[r_t1])
                    S.op("dve", lambda e, h=h: e.tensor_tensor(out=bT[h][0:H, :], in0=t1[0:H, :], in1=Winv[0:H, :], op=ALU.mult), reads=[r_t1, r_Winv], writes=[r_bT[h]])
                    S.op("dve", lambda e, h=h: e.tensor_tensor(out=rT[h][0:H, :], in0=rs[0:H, :], in1=Wi[h][0:H, :], op=ALU.mult), reads=[r_rs, r_Wi[h]], writes=[r_rT[h]])
                    S.op("dve", lambda e, s_=PV[6]: e.tensor_scalar(out=t1[0:H, :], in0=alpha[0:H, :], scalar1=-1.0, scalar2=s_, op0=ALU.add, op1=ALU.mult), reads=[r_alpha, r_pf], writes=[r_t1])
                    S.op("dve", lambda e: e.scalar_tensor_tensor(out=km[0:H, :], in0=t1[0:H, :], scalar=1.0, in1=ks[0:H, :], op0=ALU.add, op1=ALU.mult), reads=[r_t1, r_ks], writes=[r_km])
                    S.op("dve", lambda e, h=h: e.tensor_tensor(out=kTt[h][0:H, :], in0=km[0:H, :], in1=Winv[0:H, :], op=ALU.mult), reads=[r_km, r_Winv], writes=[r_kTt[h]])
                    S.op("dve", lambda e, h=h, s_=PV[7]: e.scalar_tensor_tensor(out=rkT[h][0:H, :], in0=rs[0:H, :], scalar=s_, in1=km[0:H, :], op0=ALU.mult, op1=ALU.mult), reads=[r_rs, r_km, r_pf], writes=[r_rk[h]])
                    S.op("act", lambda e, h=h: e.activation(out=vT[h][0:H, :], in_=vs[0:H, :], func=AF.Copy), reads=[r_vs], writes=[r_vT[h]])
                    for half in range(2):
                        pbv = ps[2][:, :].bitcast(BF16).rearrange("p (c a d) -> p c a d", a=4, d=64)
                        for c4 in range(4):
                            c = 4 * half + c4
                            for ai, (arr, rr) in enumerate(((aT[h], r_aT[h]), (bT[h], r_bT[h]), (kTt[h], r_kTt[h]), (vT[h], r_vT[h]))):
                                S.op("pe", lambda e, c4=c4, c=c, ai=ai, arr=arr, pbv=pbv: e.transpose(out=pbv[0:H, c4, ai, :], in_=arr[0:H, 64 * c:64 * c + 64], identity=ident_b[0:H, 0:H]),
                                     reads=[rr, r_cb], writes=[rps[2]])
                        S.op("act", lambda e, h=h, half=half, pbv=pbv: e.activation(out=tm[h][0:H, 4 * half:4 * half + 4, :, :], in_=pbv[0:H, :, :, :], func=AF.Copy), reads=[rps[2]], writes=[r_tm[h]])
                for du in range(4):
                    cs = (2 * du, 2 * du + 1)
                    probs = [(h, q, 2 * h + q, cs[q]) for h in range(2) for q in range(2)]
                    csl = [slice(64 * cs[0], 64 * cs[0] + 64), slice(64 * cs[1], 64 * cs[1] + 64)]
                    for h in range(2):
                        bk = 3 if h == 0 else 2
                        pP = ps[bk][0:H, :].rearrange("p (q m t) -> p q m t", q=2, m=4)
                        for q in range(2):
                            for m, (l_, r_, rl, rr) in enumerate(((bT[h], aT[h], r_bT[h], r_aT[h]), (kTt[h], aT[h], r_kTt[h], r_aT[h]), (bT[h], rT[h], r_bT[h], r_rT[h]), (kTt[h], rT[h], r_kTt[h], r_rT[h]))):
                                S.op("pe", lambda e, q=q, m=m, l_=l_, r_=r_, pP=pP, sl=csl[q]: e.matmul(pP[:, q, m, :], lhsT=l_[0:H, sl], rhs=r_[0:H, sl], start=True, stop=True),
                                     reads=[rl, rr], writes=[rps[bk]])
                        S.op("dve", lambda e, h=h, bk=bk: e.tensor_tensor(out=Pm[0:H, 2 * h:2 * h + 2].rearrange("p q m t -> p (q m t)"), in0=ps[bk][0:H, :], in1=maskP[0:H, :], op=ALU.mult),
                             reads=[rps[bk], r_cb], writes=[r_Pm])
                    pD = ps[4][0:H, :].rearrange("p (q s t) -> p q s t", q=4, s=2)
                    for (h, q, p4, c) in probs:
                        S.op("pe", lambda e, p4=p4, h=h, sl=csl[q]: e.matmul(pD[:, p4, 0, :], lhsT=aT[h][0:H, sl], rhs=bT[h][0:H, sl], start=True, stop=True), reads=[r_aT[h], r_bT[h]], writes=[rps[4]])
                    S.op("dve", lambda e: e.tensor_tensor(out=Dk[0][0:H, :, 0, :], in0=pD[:, :, 0, :], in1=maskN0[0:H, 0:64].unsqueeze(1).to_broadcast([H, 4, 64]), op=ALU.mult),
                         reads=[rps[4], r_cb], writes=[r_Dk[0]])
                    S.op("act", lambda e: e.activation(out=Dk[0][0:H, :, 1, :], in_=Pm[0:H, :, 0, :], func=AF.Copy), reads=[r_Pm], writes=[r_Dk[0]])
                    pX = ps[5][0:H, :].rearrange("p (q d) -> p q d", q=4)
                    for (h, q, p4, c) in probs:
                        S.op("pe", lambda e, p4=p4, h=h, c=c: e.matmul(pX[:, p4, 0:64], lhsT=Pm[0:H, p4, 1, :], rhs=tm[h][0:H, c, 3, :], start=True, stop=True), reads=[r_Pm, r_tm[h]], writes=[rps[5]])
                    S.op("act", lambda e: e.activation(out=Xf[0:H, :, 0:64], in_=pX[:, :, 0:64], func=AF.Copy), reads=[rps[5]], writes=[r_Xf])
                    for h in range(2):
                        S.op("dve", lambda e, h=h, c0_=cs[0]: e.tensor_copy(out=Xf[0:H, 2 * h:2 * h + 2, 64:128], in_=tm[h][0:H, c0_:c0_ + 2, 0, :]), reads=[r_tm[h]], writes=[r_Xf])
                    S.op("act", lambda e: e.activation(out=Xb[0:H], in_=Xf[0:H], func=AF.Copy), reads=[r_Xf], writes=[r_Xb])
                    for lv in range(6):
                        cu = lv % 2
                        nx = 1 - cu
                        for p4 in range(4):
                            S.op("pe", lambda e, p4=p4, cu=cu: e.matmul(pX[:, p4, :], lhsT=Dk[cu][0:H, p4, 1, :], rhs=Xb[0:H, p4, :], start=True, stop=True),
                                 reads=[r_Dk[cu], r_Xb], writes=[rps[5]])
                        S.op("dve", lambda e: e.tensor_tensor(out=Xf[0:H], in0=Xf[0:H], in1=pX, op=ALU.add), reads=[rps[5], r_Xf], writes=[r_Xf])
                        S.op("act", lambda e: e.activation(out=Xb[0:H], in_=Xf[0:H], func=AF.Copy), reads=[r_Xf], writes=[r_Xb])
                        if lv < 5:
                            for p4 in range(4):
                                S.op("pe", lambda e, p4=p4, cu=cu: e.matmul(pD[:, p4, 0, :], lhsT=Dk[cu][0:H, p4, 1, :], rhs=Dk[cu][0:H, p4, 0, :], start=True, stop=True),
                                     reads=[r_Dk[cu]], writes=[rps[4]])
                                S.op("pe", lambda e, p4=p4, cu=cu: e.matmul(pD[:, p4, 1, :], lhsT=Dk[cu][0:H, p4, 0, :], rhs=Dk[cu][0:H, p4, 1, :], start=True, stop=True),
                                     reads=[r_Dk[cu]], writes=[rps[4]])
                            S.op("act", lambda e, nx=nx: e.activation(out=Dk[nx][0:H].rearrange("p q s t -> p (q s t)"), in_=ps[4][0:H, :], func=AF.Copy), reads=[rps[4]], writes=[r_Dk[nx]])
                    pRG = ps[6][0:H, :].rearrange("p (k q t) -> p k q t", k=2, q=4)
                    pHY = ps[0][0:H, :].rearrange("p (k q t) -> p k q t", k=2, q=4)
                    for (h, q, p4, c) in probs:
                        S.op("pe", lambda e, p4=p4: e.matmul(pRG[:, 0, p4, :], lhsT=Xb[0:H, p4, 64:128], rhs=Pm[0:H, p4, 2, :], start=True, stop=True), reads=[r_Xb, r_Pm], writes=[rps[6]])
                        S.op("pe", lambda e, p4=p4, h=h, c=c: e.matmul(pRG[:, 1, p4, :], lhsT=Xb[0:H, p4, 64:128], rhs=tm[h][0:H, c, 1, :], start=True, stop=True), reads=[r_Xb, r_tm[h]], writes=[rps[6]])
                        S.op("pe", lambda e, p4=p4, h=h, c=c: e.matmul(pHY[:, 0, p4, :], lhsT=tm[h][0:H, c, 1, :], rhs=Xb[0:H, p4, 0:64], start=True, stop=False), reads=[r_Xb, r_tm[h]], writes=[rps[0]])
                        S.op("pe", lambda e, p4=p4, h=h, c=c: e.matmul(pHY[:, 0, p4, :], lhsT=tm[h][0:H, c, 2, :], rhs=tm[h][0:H, c, 3, :], start=False, stop=True), reads=[r_tm[h]], writes=[rps[0]])
                        S.op("pe", lambda e, p4=p4: e.matmul(pHY[:, 1, p4, :], lhsT=Pm[0:H, p4, 2, :], rhs=Xb[0:H, p4, 0:64], start=True, stop=False), reads=[r_Xb, r_Pm], writes=[rps[0]])
                        S.op("pe", lambda e, p4=p4, h=h, c=c: e.matmul(pHY[:, 1, p4, :], lhsT=Pm[0:H, p4, 3, :], rhs=tm[h][0:H, c, 3, :], start=False, stop=True), reads=[r_Pm, r_tm[h]], writes=[rps[0]])
                    for h in range(2):
                        S.op("dve", lambda e, h=h, c0_=cs[0]: e.tensor_tensor(out=RhT[0:H, 2 * h:2 * h + 2, :], in0=pRG[:, 0, 2 * h:2 * h + 2, :], in1=rT[h][0:H, 64 * c0_:64 * c0_ + 128].rearrange("p (q t) -> p q t", q=2), op=ALU.add),
                             reads=[rps[6], r_rT[h]], writes=[r_Rh])
                    S.op("dve", lambda e: e.tensor_tensor(out=GpT[0:H], in0=pRG[:, 1, :, :], in1=ident_f[0:H, 0:H].unsqueeze(1).to_broadcast([H, 4, 64]), op=ALU.add),
                         reads=[rps[6], r_cf], writes=[r_Gp])
                    for (h, q, p4, c) in probs:
                        ce = 64 * c + 63
                        S.op("dve", lambda e, p4=p4, h=h, ce=ce: e.tensor_scalar(out=HmW[0:H, p4, :], in0=pHY[:, 0, p4, :], scalar1=Wi[h][0:H, ce:ce + 1], scalar2=None, op0=ALU.mult),
                             reads=[rps[0], r_Wi[h]], writes=[r_Hm])
                    S.op("act", lambda e: e.activation(out=Y0[0:H], in_=pHY[:, 1, :, :], func=AF.Copy), reads=[rps[0]], writes=[r_Y0])
                    for q in range(2):
                        for h in range(2):
                            p4 = 2 * h + q
                            c = cs[q]
                            ce = 64 * c + 63
                            bk = 7 if h == 0 else 1
                            S.op("pe", lambda e, p4=p4, h=h, bk=bk: e.matmul(ps[bk][0:H, 0:64], lhsT=RhT[0:H, p4, :], rhs=ST[h][0:H, :], start=True, stop=True), reads=[r_Rh, r_ST[h]], writes=[rps[bk]])
                            S.op("pe", lambda e, p4=p4, h=h, bk=bk: e.matmul(ps[bk][0:H, 64:128], lhsT=GpT[0:H, p4, :], rhs=ST[h][0:H, :], start=True, stop=True), reads=[r_Gp, r_ST[h]], writes=[rps[bk]])
                            S.op("dve", lambda e, p4=p4, h=h, c=c, bk=bk: e.tensor_tensor(out=Ytok[h][0:H, c, :], in0=ps[bk][0:H, 0:64], in1=Y0[0:H, p4, :], op=ALU.add), reads=[rps[bk], r_Y0], writes=[r_Yt[h]])
                            S.op("dve", lambda e, p4=p4, h=h, ce=ce, bk=bk: e.scalar_tensor_tensor(out=ST[h][0:H, :], in0=ps[bk][0:H, 64:128], scalar=Wi[h][0:H, ce:ce + 1], in1=HmW[0:H, p4, :], op0=ALU.mult, op1=ALU.add),
                                 reads=[rps[bk], r_Wi[h], r_Hm], writes=[r_ST[h]])
                for h in range(2):
                    hh = 2 * j + h
                    cw = 64 * hh
                    S.op("dve", lambda e, h=h: e.tensor_reduce(out=st1[0:H, :], in_=Ytok[h][0:H], axis=AX.X, op=ALU.add), reads=[r_Yt[h]], writes=[r_st1])
                    S.op("dve", lambda e: e.tensor_scalar(out=st1[0:H, :], in0=st1[0:H, :], scalar1=1.0 / 64.0, scalar2=None, op0=ALU.mult), reads=[r_st1], writes=[r_st1])
                    S.op("dve", lambda e, h=h: e.tensor_tensor(out=yc[0:H], in0=Ytok[h][0:H], in1=st1[0:H, :].unsqueeze(2).to_broadcast([H, 8, 64]), op=ALU.subtract), reads=[r_Yt[h], r_st1], writes=[r_yc])
                    S.op("act", lambda e: e.activation(out=ysq[0:H], in_=yc[0:H], func=AF.Square), reads=[r_yc], writes=[r_ysq])
                    S.op("dve", lambda e: e.tensor_reduce(out=st2[0:H, :], in_=ysq[0:H], axis=AX.X, op=ALU.add), reads=[r_ysq], writes=[r_st2])
                    S.op("dve", lambda e: e.tensor_scalar(out=st2[0:H, :], in0=st2[0:H, :], scalar1=1.0 / 64.0, scalar2=64e-5, op0=ALU.mult, op1=ALU.add), reads=[r_st2], writes=[r_st2])
                    S.op("act", lambda e: e.activation(out=st2[0:H, :], in_=st2[0:H, :], func=AF.Sqrt), reads=[r_st2], writes=[r_st2])
                    S.op("dve", lambda e: e.reciprocal(out=st2[0:H, :], in_=st2[0:H, :]), reads=[r_st2], writes=[r_st2])
                    S.op("dve", lambda e: e.tensor_tensor(out=yc[0:H], in0=yc[0:H], in1=st2[0:H, :].unsqueeze(2).to_broadcast([H, 8, 64]), op=ALU.mult), reads=[r_yc, r_st2], writes=[r_yc])
                    S.op("dve", lambda e, cw=cw: e.tensor_tensor(out=yc[0:H], in0=yc[0:H], in1=lnw_bc[0:H, cw:cw + 64].unsqueeze(1).to_broadcast([H, 8, 64]), op=ALU.mult), reads=[r_yc, r_rowb], writes=[r_yc])
                    S.op("dve", lambda e, cw=cw: e.tensor_tensor(out=yc[0:H], in0=yc[0:H], in1=lnb_bc[0:H, cw:cw + 64].unsqueeze(1).to_broadcast([H, 8, 64]), op=ALU.add), reads=[r_yc, r_rowb], writes=[r_yc])
                    for c in range(8):
                        S.op("pe", lambda e, c=c, h=h: e.matmul(ps[0][0:H, c:c + 1], lhsT=rkT[h][0:H, 64 * c:64 * c + 64], rhs=ones_f[0:H, 0:1], start=True, stop=True), reads=[r_rk[h], r_cf], writes=[rps[0]])
                    S.op("act", lambda e: e.activation(out=bon[0:H, :], in_=ps[0][0:H, 0:8], func=AF.Copy), reads=[rps[0]], writes=[r_bon])
                    S.op("dve", lambda e, h=h: e.tensor_tensor(out=ysq[0:H], in0=tm[h][0:H, :, 3, :], in1=bon[0:H, :].unsqueeze(2).to_broadcast([H, 8, 64]), op=ALU.mult), reads=[r_tm[h], r_bon], writes=[r_ysq])
                    S.op("dve", lambda e: e.tensor_tensor(out=yc[0:H], in0=yc[0:H], in1=ysq[0:H], op=ALU.add), reads=[r_yc, r_ysq], writes=[r_yc])
                    for c in range(8):
                        S.op("pe", lambda e, c=c: e.transpose(out=ps[1][0:H, 64 * c:64 * c + 64], in_=yc[0:H, c, :], identity=ident_f[0:H, 0:H]), reads=[r_yc, r_cf], writes=[rps[1]])
                    yi = h
                    S.op("dve", lambda e, yi=yi, h=h: e.tensor_tensor(out=ybuf[yi][0:H, :], in0=ps[1][0:H, :], in1=sgT[h][0:H, :], op=ALU.mult), reads=[rps[1], r_sg[h]], writes=[r_yb[yi]])
                    row0 = 1024 * gsel["g"] + 128 * j + 64 * h
                    S.dma("pool", lambda e, yi=yi, row0=row0, t0=t0: e.dma_start(out=yT_d[row0:row0 + 64, t0:t0 + 512], in_=ybuf[yi][0:H, :]), reads=[r_yb[yi]], writes=[r_yTd])
            S.barrier()

        for g in range(G):
            if g > 0:
                S.barrier()
                load_params(g)
            unit0()
            if stage >= 1:
                for j in range(3):
                    fox_unit(j)
            if stage >= 2:
                mem_unit()
            if stage >= 3:
                for j in range(3):
                    rwkv_unit(j)
        if FUSED:
            yv = yT_d.rearrange("(kc p) t -> p kc t", p=128)
            for qtr in range(4):
                S.barrier()
                A.off = PERS
                out_phase(nc, S, A, ps, rps, (lambda qt, qtr=qtr: yv[:, :, 1024 * qtr + 256 * qt:1024 * qtr + 256 * qt + 256]), r_yTd, wout,
                          xfull[1024 * qtr:1024 * qtr + 1024, :], gpost, out[1024 * qtr:1024 * qtr + 1024, :])
        S.barrier()

        _replay(nc, S, st)
    return nc


def _unit_cols(g):
    u0 = list(range(4608, 4864)) + [11008 + 6 * g + i for i in range(6)]
    units = [u0]
    for j in range(3):
        c0 = 384 * g + 128 * j
        units.append(list(range(6400 + c0, 6400 + c0 + 128)) + list(range(7936 + c0, 7936 + c0 + 128))
                     + list(range(11032 + c0, 11032 + c0 + 128)) + list(range(9472 + c0, 9472 + c0 + 128)))
    units.append(list(range(12568 + 256 * g, 12568 + 256 * g + 256)) + list(range(13592 + 256 * g, 13592 + 256 * g + 256)))
    for j in range(3):
        c0 = 384 * g + 128 * j
        units.append(list(range(c0, c0 + 128)) + list(range(1536 + c0, 1536 + c0 + 128)) + list(range(3072 + c0, 3072 + c0 + 128)) + list(range(4864 + c0, 4864 + c0 + 128)))
    return units


def _kc_layout(w):
    n = w.shape[1]
    return np.ascontiguousarray(w.reshape(NKC, 128, n).transpose(1, 0, 2).reshape(128, NKC * n))


def _consts():
    cf = np.zeros((128, 896), np.float32)
    cf[:, 0:128] = np.eye(128)
    sp = np.arange(128)
    cf[:, 128:256] = (sp[:, None] <= sp[None, :])
    cf[:, 256:384] = 1.0
    rm = np.ones(512, np.float32)
    rm[::64] = 0.0
    cf[:, 384:896] = rm[None, :]
    cb = np.zeros((128, 1024), np.float32)
    cb[:, 0:128] = np.eye(128)
    cb[:, 128:256] = 1.0
    cb[:, 256:384] = (sp[None, :] >= sp[:, None])
    j = (sp % 64)[:, None]
    t = np.arange(64)[None, :]
    strict = (t > j).astype(np.float32)
    incl = (t >= j).astype(np.float32)
    mp = np.stack([strict, strict, incl, incl], axis=1)
    cb[:, 384:896] = np.concatenate([mp, mp], axis=1).reshape(128, 512)
    n0 = (t < j).astype(np.float32)
    cb[:, 896:1024] = np.concatenate([n0, n0], axis=1)
    return cf, cb


def prep_inputs(inp):
    f = lambda a: np.ascontiguousarray(np.asarray(a, dtype=np.float32))
    x = f(inp["x"]); mem = f(inp["mem"])
    w_in = f(inp["w_in"])[0]
    cf, cb = _consts()
    mu = f(inp["mu_rwkv"])[0]
    w0 = f(inp["w0"])[0]; a0 = f(inp["a0"])[0]; k_k = f(inp["k_k"])[0]; k_a = f(inp["k_a"])[0]
    r_k = f(inp["r_k"])[0].reshape(-1)
    lnw = f(inp["ln_x_w"])[0]; lnb = f(inp["ln_x_b"])[0]; b_f = f(inp["b_f"])[0]
    g_pre = f(inp["g_pre"])[0]; g_mem = f(inp["g_mem"])[0]; g_post = f(inp["g_post"])[0]
    wdu = f(inp["w_decay_up"])[0]; wiu = f(inp["w_iclr_up"])[0]
    wkv_full = f(inp["w_mem_kv"])[0]; w_out = f(inp["w_out"])[0]
    xT = [np.ascontiguousarray(x[b].T) for b in range(2)]
    memT = [np.ascontiguousarray(mem[b].T) for b in range(2)]
    per_g = []
    for g in range(4):
        units = _unit_cols(g)
        wc = np.concatenate([_kc_layout(w_in[:, u]) for u in units], axis=1)
        wkv = _kc_layout(np.concatenate([wkv_full[:, 256 * g:256 * g + 256], wkv_full[:, 1024 + 256 * g:1024 + 256 * g + 256]], axis=1))
        prm = np.zeros((128, 114), np.float32)
        prm[:, 0:32] = g_pre.reshape(NKC, 128).T
        prm[:, 32:64] = g_mem.reshape(NKC, 128).T
        prm[:, 64] = mu[4608:4736]
        prm[:, 65] = mu[4736:4864]
        for hh in range(6):
            c = 384 * g + 64 * hh
            base = 66 + 8 * hh
            for k, v in enumerate((mu[c:c + 64], mu[1536 + c:1536 + c + 64], mu[3072 + c:3072 + c + 64], w0[c:c + 64], a0[c:c + 64], k_k[c:c + 64], k_a[c:c + 64], r_k[c:c + 64])):
                prm[0:64, base + k] = v
        rowp = np.concatenate([lnw[384 * g:384 * g + 384], lnb[384 * g:384 * g + 384], b_f[6 * g:6 * g + 6]])[None, :].astype(np.float32)
        wlora = np.ascontiguousarray(np.concatenate([wdu[:, 384 * g:384 * g + 384], wiu[:, 384 * g:384 * g + 384]], axis=1))
        per_g.append(dict(wc=wc, wkv=wkv, prm=prm, rowp=np.ascontiguousarray(rowp), wlora=wlora))
    perm = []
    for g in range(4):
        perm += list(range(384 * g, 384 * g + 384)) + list(range(1536 + 384 * g, 1536 + 384 * g + 384)) + list(range(3072 + 256 * g, 3072 + 256 * g + 256))
    wo_p = w_out[np.array(perm), :]
    wout = np.stack([_kc_layout(wo_p[:, 512 * cb_:512 * cb_ + 512]) for cb_ in range(8)], axis=0)
    in_maps = []
    for c in range(8):
        b, g = c // 4, c % 4
        m = dict(xT=xT[b], xrows=np.ascontiguousarray(x[b, 1024 * g:1024 * g + 1024, :]), memT=memT[b], cstf=cf, cstb=cb,
                 gpost=np.ascontiguousarray(g_post[None, :]), wout=wout)
        m.update(per_g[g])
        in_maps.append(m)
    return in_maps


def _replay(nc, S, st):
    block = st.enter_context(nc.Block())
    names = {"pe": "tensor", "act": "scalar", "dve": "vector", "pool": "gpsimd", "sp": "sync"}
    for e in ENGS:
        ops = S.ops[e]
        if not ops:
            continue

        def run(en, ops=ops):
            for o in ops:
                o(en)
        getattr(block, names[e])(run)


def out_phase(nc, S, A, ps, rps, ysrc_of_quarter, r_ysrc, wout, xrows, gpost, out):
    Wo = [A.bf16(NKC * 512).rearrange("p (kc n) -> p kc n", n=512) for _ in range(2)]
    r_Wo = [Res(), Res()]
    ytb = A.bf16(NKC * 256).rearrange("p (kc t) -> p kc t", t=256); r_ytb = [Res(), Res()]
    yo = A.f32(2 * DM).rearrange("p (i n) -> p i n", n=DM); r_yo = Res()
    gpb = A.f32(DM); r_gpb = Res()
    xr = [A.f32(2048) for _ in range(2)]; r_xr = [Res(), Res()]
    ssq = A.f32(16); r_ssq = Res()
    rst = A.f32(2); r_rst = Res()
    junk = A.f32(512); r_junk = Res()
    S.dma("sp", lambda e: e.dma_start(out=gpb, in_=gpost[0:1, :].to_broadcast([128, DM])), writes=[r_gpb])
    wcount = 0
    xcount = 0
    for qt in range(4):
        ysrc = ysrc_of_quarter(qt)
        for hf in range(2):
            S.dma("sp", lambda e, hf=hf, ysrc=ysrc: e.dma_start(out=ytb[:, 16 * hf:16 * hf + 16, :], in_=ysrc[:, 16 * hf:16 * hf + 16, :]), reads=[r_ysrc], writes=[r_ytb[hf]])
        for cb in range(8):
            wb = wcount % 2
            wcount += 1
            for o in range(0, NKC * 512, 2048):
                S.dma("pool", lambda e, wb=wb, cb=cb, o=o: e.dma_start(out=Wo[wb].rearrange("p kc n -> p (kc n)")[:, o:o + 2048], in_=wout[cb, :, o:o + 2048]), writes=[r_Wo[wb]])
            for i in range(2):
                pb = (cb * 2 + i) % 4
                for kc in range(NKC):
                    S.op("pe", lambda e, wb=wb, i=i, kc=kc, pb=pb: e.matmul(ps[pb][:, :], lhsT=ytb[:, kc, 128 * i:128 * i + 128], rhs=Wo[wb][:, kc, :], start=(kc == 0), stop=(kc == NKC - 1)),
                         reads=[r_Wo[wb], r_ytb[kc // 16]], writes=[rps[pb]])
                S.op("dve", lambda e, i=i, cb=cb, pb=pb: e.tensor_copy(out=yo[:, i, 512 * cb:512 * cb + 512], in_=ps[pb][:, :]), reads=[rps[pb]], writes=[r_yo])
                S.op("act", lambda e, pb=pb: e.activation(out=junk, in_=ps[pb][:, :], func=AF.Square), reads=[rps[pb]], writes=[r_junk])
                S.op("dve", lambda e, i=i, cb=cb: e.tensor_reduce(out=ssq[:, 8 * i + cb:8 * i + cb + 1], in_=junk, axis=AX.X, op=ALU.add), reads=[r_junk], writes=[r_ssq])
        for i in range(2):
            S.op("dve", lambda e, i=i: e.tensor_reduce(out=rst[:, i:i + 1], in_=ssq[:, 8 * i:8 * i + 8], axis=AX.X, op=ALU.add), reads=[r_ssq], writes=[r_rst])
            S.op("dve", lambda e, i=i: e.tensor_scalar(out=rst[:, i:i + 1], in0=rst[:, i:i + 1], scalar1=1.0 / DM, scalar2=1e-6, op0=ALU.mult, op1=ALU.add), reads=[r_rst], writes=[r_rst])
            S.op("act", lambda e, i=i: e.activation(out=rst[:, i:i + 1], in_=rst[:, i:i + 1], func=AF.Sqrt), reads=[r_rst], writes=[r_rst])
            S.op("dve", lambda e, i=i: e.reciprocal(out=rst[:, i:i + 1], in_=rst[:, i:i + 1]), reads=[r_rst], writes=[r_rst])
            for hc in range(2):
                xi = xcount % 2
                xcount += 1
                r0 = 256 * qt + 128 * i
                cs_ = slice(2048 * hc, 2048 * hc + 2048)
                S.dma("sp", lambda e, xi=xi, r0=r0, cs_=cs_: e.dma_start(out=xr[xi], in_=xrows[r0:r0 + 128, cs_]), writes=[r_xr[xi]])
                S.op("dve", lambda e, i=i, cs_=cs_: e.scalar_tensor_tensor(out=yo[:, i, cs_], in0=yo[:, i, cs_], scalar=rst[:, i:i + 1], in1=gpb[:, cs_], op0=ALU.mult, op1=ALU.mult),
                     reads=[r_yo, r_rst, r_gpb], writes=[r_yo])
                S.op("dve", lambda e, i=i, cs_=cs_, xi=xi: e.tensor_tensor(out=xr[xi], in0=xr[xi], in1=yo[:, i, cs_], op=ALU.add), reads=[r_yo, r_xr[xi]], writes=[r_xr[xi]])
                S.dma("pool", lambda e, xi=xi, r0=r0, cs_=cs_: e.dma_start(out=out[r0:r0 + 128, cs_], in_=xr[xi]), reads=[r_xr[xi]])


def build_out_program():
    nc = bass.Bass("TRN2", target_bir_lowering=False)
    dr = lambda name, shape, dt, kind="ExternalInput": nc.dram_tensor(name, shape, dt, kind=kind).ap()
    ysel = dr("ysel", [DM, 1024], BF16)
    xrows = dr("xrows", [1024, DM], F32)
    wout = dr("wout", [8, 128, NKC * 512], F32)
    gpost = dr("gpost", [1, DM], F32)
    out = dr("out", [1024, DM], F32, kind="ExternalOutput")
    S = Sched(nc)
    with contextlib.ExitStack() as st:
        for e in ("pe", "act", "dve", "pool"):
            S.sem[e] = st.enter_context(nc.semaphore("s_" + e))
        for i in range(40):
            S.dma_sems.append([st.enter_context(nc.semaphore("d%d" % i)), 0])
        arena_t = st.enter_context(nc.sbuf_tensor("arena", [128, 43008], F32))
        ps = [st.enter_context(nc.psum_tensor("ps%d" % i, [128, 512], F32)) for i in range(8)]
        rps = [Res("ps%d" % i) for i in range(8)]
        A = Arena(arena_t[:, :])
        yv = ysel.rearrange("(kc p) t -> p kc t", p=128)
        out_phase(nc, S, A, ps, rps, lambda qt: yv[:, :, 256 * qt:256 * qt + 256], Res(), wout, xrows, gpost, out)
        S.barrier()
        _replay(nc, S, st)
    return nc


_PROGS = {}


def kernel(**inputs):
    in_maps = prep_inputs(inputs)
    if "a" not in _PROGS:
        _PROGS["a"] = build_program(stage=3, G=1)
    keys_a = ("xT", "memT", "wc", "wkv", "cstf", "cstb", "prm", "rowp", "wlora")
    lead = ("wc", "wkv", "prm", "rowp", "wlora")
    res_a = run_bass_kernel_spmd(_PROGS["a"], [{k: (m[k][None] if k in lead else m[k]) for k in keys_a} for m in in_maps], core_ids=list(range(8)))
    yT = [np.asarray(r["yT_d"]) for r in res_a.results]
    maps_b = []
    for c in range(8):
        b, g = c // 4, c % 4
        ysel = np.ascontiguousarray(np.concatenate([yT[4 * b + gg][:, 1024 * g:1024 * g + 1024] for gg in range(4)], axis=0))
        maps_b.append(dict(ysel=ysel, xrows=in_maps[c]["xrows"], wout=in_maps[c]["wout"], gpost=in_maps[c]["gpost"]))
    if "b" not in _PROGS:
        _PROGS["b"] = build_out_program()
    res_b = run_bass_kernel_spmd(_PROGS["b"], maps_b, core_ids=list(range(8)))
    outp = np.empty((2, T, DM), np.float32)
    for c in range(8):
        b, g = c // 4, c % 4
        outp[b, 1024 * g:1024 * g + 1024, :] = np.asarray(res_b.results[c]["out"])
    return outp
```
